# Optimizing a Trainium2 kernel written in Bass

```python
import math
import jax
import jax.numpy as jnp
from jax import lax
import numpy as np

D_MODEL = 2048
BATCH = 2
SEQ = 16384
DEPTH = 4

GRID_W = 64
CTX_LEN = 256
D_MIX = D_MODEL
HEAD_DIM = 128
NA_WIDTH = D_MIX // 2
NA_HEADS = NA_WIDTH // HEAD_DIM
NA_KH = 8
NA_KW = 16
FN_WIDTH = D_MIX // 4
FN_GROUPS = 4
FN_GROUP_DIM = FN_WIDTH // FN_GROUPS
HY_WIDTH = D_MIX - NA_WIDTH - FN_WIDTH
HY_ORDER = 2
HY_BANDS = 16
HY_EMB = 1 + 2 * HY_BANDS
HY_FILTER_HIDDEN = 64
HY_FAST_DECAY = 0.3
HY_SLOW_DECAY = 1.5
HY_TARGET = 1e-2
HY_MIN_DECAY = math.log(HY_TARGET) / HY_SLOW_DECAY
HY_MAX_DECAY = math.log(HY_TARGET) / HY_FAST_DECAY
CONV_W = 3
D_FF = 11 * D_MODEL // 4
D_IN = 3 * NA_WIDTH + FN_WIDTH + (HY_ORDER + 1) * HY_WIDTH
N_MOD = 6
EPS = 1e-6

kernel_name = "hybrid_na_fourier_hyena_dit"


def rmsnorm(x, g):
    xf = x.astype(jnp.float32)
    y = xf * lax.rsqrt(jnp.mean(xf * xf, axis=-1, keepdims=True) + EPS)
    return (y * g.astype(jnp.float32)).astype(x.dtype)


def modulate(h, shift, scale):
    return h * (1 + scale) + shift


def dwconv_centred(x, w, b):
    L = x.shape[1]
    pad = CONV_W // 2
    xp = jnp.pad(x, ((0, 0), (pad, pad), (0, 0)))
    return sum(xp[:, j:j + L] * w[j] for j in range(CONV_W)) + b


def heads(t):
    return t.reshape(*t.shape[:-1], NA_HEADS, HEAD_DIM)


def split_projection(p):
    o1, o2, o3 = NA_WIDTH, 2 * NA_WIDTH, 3 * NA_WIDTH
    o4 = o3 + FN_WIDTH
    return p[..., :o1], p[..., o1:o2], p[..., o2:o3], p[..., o3:o4], p[..., o4:]


def neighbourhood_attention(q, k, v, kc, vc, rpb):
    B, L, H, Dh = q.shape
    rows = L // GRID_W
    kh = min(NA_KH, rows)
    n_loc = kh * NA_KW
    scale = Dh ** -0.5
    col = np.arange(GRID_W)
    col_start = np.clip(col - NA_KW // 2, 0, GRID_W - NA_KW)
    col_idx = col_start[:, None] + np.arange(NA_KW)[None, :]
    dcol = col_idx - col[:, None] + (NA_KW - 1)
    rpb_cols = rpb[:, :, dcol]
    kg = k.reshape(B, rows, GRID_W, H, Dh)
    vg = v.reshape(B, rows, GRID_W, H, Dh)
    q_rows = q.reshape(B, rows, GRID_W, H, Dh).transpose(1, 0, 2, 3, 4)

    def row_block(args):
        r, q_r = args
        r0 = jnp.clip(r - kh // 2, 0, rows - kh)
        k_rows = lax.dynamic_slice_in_dim(kg, r0, kh, axis=1)
        v_rows = lax.dynamic_slice_in_dim(vg, r0, kh, axis=1)
        k_win = jnp.take(k_rows, col_idx, axis=2)
        v_win = jnp.take(v_rows, col_idx, axis=2)
        drow = r0 + jnp.arange(kh) - r + (NA_KH - 1)
        bias = jnp.take(rpb_cols, drow, axis=1).transpose(0, 2, 1, 3)
        s_loc = jnp.einsum("bqhd,biqjhd->bhqij", q_r, k_win).astype(jnp.float32) * scale
        s_loc = s_loc + bias.astype(jnp.float32)
        s_ctx = jnp.einsum("bqhd,bchd->bhqc", q_r, kc).astype(jnp.float32) * scale
        s = jnp.concatenate([s_loc.reshape(B, H, GRID_W, n_loc), s_ctx], axis=-1)
        p = jax.nn.softmax(s, axis=-1).astype(v.dtype)
        p_loc = p[..., :n_loc].reshape(B, H, GRID_W, kh, NA_KW)
        return (jnp.einsum("bhqij,biqjhd->bqhd", p_loc, v_win)
                + jnp.einsum("bhqc,bchd->bqhd", p[..., n_loc:], vc))

    out = lax.map(row_block, (jnp.arange(rows), q_rows))
    return out.transpose(1, 0, 2, 3, 4).reshape(B, L, H * Dh)


def context_attention(qc, kc, vc):
    B, Lc, H, Dh = qc.shape
    s = jnp.einsum("bqhd,bkhd->bhqk", qc, kc).astype(jnp.float32) * (Dh ** -0.5)
    p = jax.nn.softmax(s, axis=-1).astype(vc.dtype)
    return jnp.einsum("bhqk,bkhd->bqhd", p, vc).reshape(B, Lc, H * Dh)


def fourier_mix(f):
    B, L, _ = f.shape
    fg = f.astype(jnp.float32).reshape(B, L, FN_GROUPS, FN_GROUP_DIM)
    y = jnp.fft.fft2(fg, axes=(1, 3), norm="ortho").real
    return y.reshape(B, L, FN_WIDTH).astype(f.dtype)


def hyena_kernel_spectrum(L, w1, b1, w2, b2, w3, freq):
    pos = jnp.arange(L, dtype=jnp.float32)[:, None]
    t = pos / max(L - 1, 1)
    bands = jnp.linspace(1e-4, HY_BANDS - 1, HY_BANDS, dtype=jnp.float32)
    ang = bands * (2.0 * math.pi / L) * pos
    z = jnp.concatenate([t, jnp.cos(ang), -jnp.sin(ang)], axis=-1)
    hdn = jnp.sin(freq * (z @ w1 + b1))
    hdn = jnp.sin(freq * (hdn @ w2 + b2))
    h = (hdn @ w3).astype(jnp.float32).reshape(L, HY_ORDER, 2, HY_WIDTH)
    deltas = jnp.linspace(HY_MIN_DECAY, HY_MAX_DECAY, HY_WIDTH, dtype=jnp.float32)
    h = h * jnp.exp(-t * jnp.abs(deltas))[:, None, None, :]
    hf, hb = h[:, :, 0], h[:, :, 1]
    kernel = jnp.concatenate([hf, jnp.zeros_like(hf[:1]), hb[:0:-1]], axis=0)
    return jnp.fft.rfft(kernel, axis=0)


def long_conv(z, k_spec, bias):
    L = z.shape[1]
    zf = z.astype(jnp.float32)
    y = jnp.fft.irfft(jnp.fft.rfft(zf, n=2 * L, axis=1) * k_spec, n=2 * L, axis=1)[:, :L]
    return (y + zf * bias).astype(z.dtype)


def hyena_mix(u, conv_w, conv_b, k_spec, bias):
    u = dwconv_centred(u, conv_w, conv_b)
    v, x1, x2 = jnp.split(u, HY_ORDER + 1, axis=-1)
    z = x1 * long_conv(v, k_spec[:, 0], bias[0])
    return x2 * long_conv(z, k_spec[:, 1], bias[1])


def merge_groups(a, f, hy, g):
    o1, o2 = NA_WIDTH, NA_WIDTH + FN_WIDTH
    return jnp.concatenate([rmsnorm(a, g[:o1]), rmsnorm(f, g[o1:o2]), rmsnorm(hy, g[o2:])], axis=-1)


def conv_gated_mlp(h, w_up, conv_w, conv_b, w_down):
    gate, up = jnp.split(h @ w_up, 2, axis=-1)
    return (jax.nn.gelu(dwconv_centred(gate, conv_w, conv_b), approximate=True) * up) @ w_down


def setup_inputs(seed: int = 0) -> dict:
    key = jax.random.key(seed)
    ks = jax.random.split(key, 26)

    def nrm(k, shape, s):
        return s * jax.random.normal(k, shape, dtype=jnp.float32)

    def gain(k, shape):
        return 1.0 + 0.02 * jax.random.normal(k, shape, dtype=jnp.float32)

    return {
        "x": nrm(ks[0], (BATCH, SEQ, D_MODEL), 1.0),
        "c": nrm(ks[1], (BATCH, D_MODEL), 1.0),
        "ctx": nrm(ks[2], (BATCH, CTX_LEN, D_MODEL), 1.0),
        "c_ctx": nrm(ks[3], (D_MODEL,), 1.0),
        "ada_w": nrm(ks[4], (DEPTH, D_MODEL, N_MOD * D_MODEL), 0.3 * D_MODEL ** -0.5),
        "ada_b": nrm(ks[5], (DEPTH, N_MOD * D_MODEL), 0.02),
        "norm1_g": gain(ks[6], (DEPTH, D_MODEL)),
        "norm2_g": gain(ks[7], (DEPTH, D_MODEL)),
        "w_in": nrm(ks[8], (DEPTH, D_MODEL, D_IN), D_MODEL ** -0.5),
        "na_rpb": nrm(ks[9], (DEPTH, NA_HEADS, 2 * NA_KH - 1, 2 * NA_KW - 1), 0.02),
        "hy_conv_w": nrm(ks[10], (DEPTH, CONV_W, (HY_ORDER + 1) * HY_WIDTH), CONV_W ** -0.5),
        "hy_conv_b": nrm(ks[11], (DEPTH, (HY_ORDER + 1) * HY_WIDTH), 0.02),
        "hy_w1": nrm(ks[12], (DEPTH, HY_EMB, HY_FILTER_HIDDEN), HY_EMB ** -0.5),
        "hy_b1": nrm(ks[13], (DEPTH, HY_FILTER_HIDDEN), 0.02),
        "hy_w2": nrm(ks[14], (DEPTH, HY_FILTER_HIDDEN, HY_FILTER_HIDDEN), HY_FILTER_HIDDEN ** -0.5),
        "hy_b2": nrm(ks[15], (DEPTH, HY_FILTER_HIDDEN), 0.02),
        "hy_w3": nrm(ks[16], (DEPTH, HY_FILTER_HIDDEN, HY_ORDER * 2 * HY_WIDTH), HY_FILTER_HIDDEN ** -0.5),
        "hy_freq": gain(ks[17], (DEPTH, HY_FILTER_HIDDEN)),
        "hy_bias": nrm(ks[18], (DEPTH, HY_ORDER, HY_WIDTH), 0.1),
        "mix_norm_g": gain(ks[19], (DEPTH, D_MIX)),
        "w_out": nrm(ks[20], (DEPTH, D_MIX, D_MODEL), D_MIX ** -0.5),
        "ffn_w_up": nrm(ks[21], (DEPTH, D_MODEL, 2 * D_FF), D_MODEL ** -0.5),
        "ffn_conv_w": nrm(ks[22], (DEPTH, CONV_W, D_FF), CONV_W ** -0.5),
        "ffn_conv_b": nrm(ks[23], (DEPTH, D_FF), 0.02),
        "ffn_w_down": nrm(ks[24], (DEPTH, D_FF, D_MODEL), D_FF ** -0.5),
        "final_norm_g": gain(ks[25], (D_MODEL,)),
    }


def reference(x, c, ctx, c_ctx, ada_w, ada_b, norm1_g, norm2_g, w_in, na_rpb,
              hy_conv_w, hy_conv_b, hy_w1, hy_b1, hy_w2, hy_b2, hy_w3, hy_freq, hy_bias,
              mix_norm_g, w_out, ffn_w_up, ffn_conv_w, ffn_conv_b, ffn_w_down, final_norm_g):
    L = x.shape[1]
    Lc = ctx.shape[1]
    xc = ctx
    s_lat = jax.nn.silu(c)
    s_ctx = jax.nn.silu(c_ctx)
    for l in range(DEPTH):
        update_ctx = l < DEPTH - 1
        mod = (s_lat @ ada_w[l] + ada_b[l])[:, None, :]
        sh1, sc1, g1, sh2, sc2, g2 = jnp.split(mod, N_MOD, axis=-1)
        mod_c = s_ctx @ ada_w[l] + ada_b[l]
        csh1, csc1, cg1, csh2, csc2, cg2 = jnp.split(mod_c, N_MOD, axis=-1)
        filt = (hy_w1[l], hy_b1[l], hy_w2[l], hy_b2[l], hy_w3[l], hy_freq[l])

        h = modulate(rmsnorm(x, norm1_g[l]), sh1, sc1)
        hc = modulate(rmsnorm(xc, norm1_g[l]), csh1, csc1)
        q, k, v, f, hy = split_projection(h @ w_in[l])
        if update_ctx:
            qc, kc, vc, fc, hyc = split_projection(hc @ w_in[l])
        else:
            kc, vc = jnp.split(hc @ w_in[l][:, NA_WIDTH:3 * NA_WIDTH], 2, axis=-1)
        kc_h, vc_h = heads(kc), heads(vc)
        a = neighbourhood_attention(heads(q), heads(k), heads(v), kc_h, vc_h, na_rpb[l])
        y = merge_groups(a, fourier_mix(f),
                         hyena_mix(hy, hy_conv_w[l], hy_conv_b[l], hyena_kernel_spectrum(L, *filt), hy_bias[l]),
                         mix_norm_g[l])
        x = x + g1 * (y @ w_out[l])
        if update_ctx:
            ac = context_attention(heads(qc), kc_h, vc_h)
            yc = merge_groups(ac, fourier_mix(fc),
                              hyena_mix(hyc, hy_conv_w[l], hy_conv_b[l], hyena_kernel_spectrum(Lc, *filt), hy_bias[l]),
                              mix_norm_g[l])
            xc = xc + cg1 * (yc @ w_out[l])

        ffn = (ffn_w_up[l], ffn_conv_w[l], ffn_conv_b[l], ffn_w_down[l])
        x = x + g2 * conv_gated_mlp(modulate(rmsnorm(x, norm2_g[l]), sh2, sc2), *ffn)
        if update_ctx:
            xc = xc + cg2 * conv_gated_mlp(modulate(rmsnorm(xc, norm2_g[l]), csh2, csc2), *ffn)
    return rmsnorm(x, final_norm_g)
```

```python
import math
import numpy as np
from concourse.bass_utils import run_bass_kernel_spmd
import concourse.bass as bass
import concourse.mybir as mybir

F32 = mybir.dt.float32
F32R = mybir.dt.float32r
BF16 = mybir.dt.bfloat16
AF = mybir.ActivationFunctionType
ALU = mybir.AluOpType
AX = mybir.AxisListType

PE, ACT, DVE, POOL, SP = "pe", "act", "dve", "pool", "sp"
EPOCH = 20000
NDMASEM = 12


class Buf:
    __slots__ = ("name", "w", "rs", "multi", "ws")

    def __init__(self, name="", multi=False):
        self.name = name
        self.w = None
        self.rs = []
        self.multi = multi
        self.ws = []


class Ins:
    __slots__ = ("eng", "fn", "deps", "signals", "dma", "idx", "sem", "val", "dsem", "prewait")

    def __init__(self, eng, fn, dma):
        self.eng = eng
        self.fn = fn
        self.deps = []
        self.signals = False
        self.dma = dma
        self.sem = None
        self.val = None
        self.prewait = None


class Prog:
    def __init__(self, nc):
        self.nc = nc
        self.q = {PE: [], ACT: [], DVE: [], POOL: [], SP: []}
        self.dma_count = {}
        self.pending = {PE: [], ACT: [], DVE: [], POOL: [], SP: []}

    def barrier(self):
        tails = []
        for eng, q in self.q.items():
            comp = [i for i in q[-1:] if not i.dma]
            lastc = None
            for i in reversed(q):
                if not i.dma:
                    lastc = i
                    break
            if lastc is not None:
                tails.append(lastc)
            nd = 0
            for i in reversed(q):
                if i.dma:
                    tails.append(i)
                    nd += 1
                    if nd >= NDMASEM:
                        break
        for t in tails:
            t.signals = True
        for eng in self.pending:
            self.pending[eng] = list(tails)

    def op(self, eng, fn, r=(), w=(), dma=False):
        ins = Ins(eng, fn, dma)
        ins.idx = len(self.q[eng])
        deps = []
        for b in r:
            if b.multi:
                deps.extend(b.ws)
            elif b.w is not None:
                deps.append(b.w)
        for b in w:
            if b.multi:
                deps.extend(b.rs)
            else:
                if b.w is not None:
                    deps.append(b.w)
                deps.extend(b.rs)
        for b in w:
            if b.multi:
                if b.rs:
                    b.ws = []
                b.ws.append(ins)
            b.w = ins
            b.rs = []
        for b in r:
            if b.w is not ins:
                b.rs.append(ins)
        if self.pending[eng]:
            deps.extend(self.pending[eng])
            self.pending[eng] = []
        seen = set()
        for d in deps:
            if d is ins or id(d) in seen:
                continue
            seen.add(id(d))
            if eng == PE and d.eng == PE and not d.dma and not dma:
                continue
            ins.deps.append(d)
            d.signals = True
        self.q[eng].append(ins)
        return ins

    def dma(self, eng, out, in_, r=(), w=(), **kw):
        return self.op(eng, lambda e: e.dma_start(out=out, in_=in_, **kw), r=r, w=w, dma=True)

    def emit(self, final_wait=()):
        nc = self.nc
        import contextlib
        with contextlib.ExitStack() as st:
            sems = {}
            for eng in (PE, ACT, DVE, POOL, SP):
                cnt = 0
                nsig = sum(1 for i in self.q[eng] if i.signals and not i.dma)
                nep = max(1, (nsig + EPOCH - 1) // EPOCH)
                esems = [st.enter_context(nc.semaphore(f"s_{eng}_{k}")) for k in range(nep)]
                for i in self.q[eng]:
                    if i.signals and not i.dma:
                        i.sem = esems[cnt // EPOCH]
                        i.val = (cnt % EPOCH) + 1
                        cnt += 1
                nd = sum(1 for i in self.q[eng] if i.dma)
                print('queue', eng, 'instr', len(self.q[eng]), 'signals', nsig, 'dmas', nd)
                if nd:
                    dsems = [st.enter_context(nc.semaphore(f"d_{eng}_{k}")) for k in range(min(NDMASEM, nd))]
                    last = [None] * len(dsems)
                    cnts = [0] * len(dsems)
                    k = 0
                    for i in self.q[eng]:
                        if i.dma:
                            s = k % len(dsems)
                            i.sem = dsems[s]
                            cnts[s] += 16
                            i.val = cnts[s]
                            i.prewait = last[s]
                            last[s] = i
                            i.signals = True
                            k += 1
            blk = st.enter_context(nc.Block())
            fw = list(final_wait)

            def body(eng_name, is_last_holder):
                def f(e):
                    waited = {}
                    for i in self.q[eng_name]:
                        ws = list(i.deps)
                        if i.prewait is not None:
                            ws.append(i.prewait)
                        for d in ws:
                            key = id(d.sem)
                            if waited.get(key, 0) >= d.val:
                                continue
                            e.wait_ge(d.sem, d.val)
                            waited[key] = d.val
                        ins = i.fn(e)
                        if i.signals:
                            ins.then_inc(i.sem, 16 if i.dma else 1)
                    if is_last_holder:
                        for d in fw:
                            e.wait_ge(d.sem, d.val)
                return f

            blk.sync(body(SP, True))
            blk.tensor(body(PE, False))
            blk.scalar(body(ACT, False))
            blk.vector(body(DVE, False))
            blk.gpsimd(body(POOL, False))


D = 2048
KD = D // 128
NA_W = 1024
NH = 8
FN_W = 512
HY_W = 512
D_IN = 5120
D_FF = 5632
N_MOD = 6
EPS = 1e-6
TN = 512


def na_row_info(r, rows):
    r0 = min(max(r - 4, 0), rows - 8)
    base = min(r0 - (r0 % 2), rows - 10)
    return r0, base, (r - r0, r0 - base)


def na_variants(rows):
    return sorted(set(na_row_info(r, rows)[2] for r in range(rows)))


def na_tables(rpb_l, rows):
    var = na_variants(rows)
    NV = len(var)
    p = np.arange(128)[:, None, None]
    t = np.arange(5)[None, :, None]
    c = np.arange(64)[None, None, :]
    kc = p % 64
    cs = np.clip(c - 8, 0, 48)
    colok = (kc >= cs) & (kc < cs + 16)
    dcol = np.clip(kc - c + 15, 0, 30)
    g = np.zeros((rpb_l.shape[0], 128, NV, 5, 64), np.float32)
    msk = np.zeros((128, NV, 5, 64), np.float32)
    for vi, (d, off) in enumerate(var):
        i = 2 * t + p // 64 - off
        ok = (i >= 0) & (i < 8) & colok
        drow = np.clip(i - d + 7, 0, 14)
        drow_b, dcol_b = np.broadcast_arrays(drow, dcol)
        vals = rpb_l[:, drow_b, dcol_b]
        g[:, :, vi] = np.where(ok[None], vals, np.float32(0.0))
        msk[:, vi] = np.where(ok, np.float32(0.0), np.float32(-20000.0))
    return g, msk


def fft_tables(Ls, LCs):
    N1 = Ls // 128
    t = {}
    n = np.arange(128)
    def pad(a):
        o = np.zeros((128, 128), np.float64)
        o[:a.shape[0], :a.shape[1]] = a
        return o
    ang1 = 2 * np.pi * np.outer(np.arange(N1), np.arange(N1)) / N1
    t["C1"] = pad(np.cos(ang1)); t["mS1"] = pad(-np.sin(ang1)); t["mC1"] = pad(-np.cos(ang1))
    a128 = 2 * np.pi * np.outer(n, n) / 128
    t["C128"] = np.cos(a128); t["S128"] = np.sin(a128)
    th = 2 * np.pi * np.outer(np.arange(128), np.arange(128)) / Ls
    t["twf"] = np.stack([np.cos(th), np.sin(th), -np.sin(th)], -1)
    n256 = np.arange(256)
    a256 = 2 * np.pi * np.outer(n256, n256) / 256
    def ch2(a):
        return a.reshape(2, 128, -1).transpose(1, 0, 2)
    t["C256"] = ch2(np.cos(a256)); t["S256"] = ch2(np.sin(a256)); t["mS256"] = ch2(-np.sin(a256)); t["mC256"] = ch2(-np.cos(a256))
    thh = 2 * np.pi * np.outer(np.arange(128), n256) / (2 * Ls)
    t["twh"] = np.stack([np.cos(thh), np.sin(thh), -np.sin(thh)], -1)
    thi = 2 * np.pi * np.outer(n256, np.arange(N1)) / (2 * Ls)
    t["twi"] = ch2(np.stack([np.cos(thi), np.sin(thi), -np.sin(thi)], -1).reshape(256, N1 * 3)).reshape(128, 2, N1, 3)
    assert LCs == 256
    ac = 2 * np.pi * np.outer(n256, n256) / LCs
    t["CLc"] = ch2(np.cos(ac)); t["mSLc"] = ch2(-np.sin(ac))
    ah = 2 * np.pi * np.outer(n256, np.arange(512)) / 512
    t["Chf"] = ch2(np.cos(ah)); t["mShf"] = ch2(-np.sin(ah))
    ai = 2 * np.pi * np.outer(np.arange(512), n256) / 512
    def ch4(a):
        return a.reshape(4, 128, -1).transpose(1, 0, 2)
    t["Chi"] = ch4(np.cos(ai)); t["mShi"] = ch4(-np.sin(ai))
    return {k_: np.ascontiguousarray(v, dtype=np.float32) for k_, v in t.items()}


def hyena_consts(Ls):
    HY_BANDS, HY_W_ = 16, 512
    pos = np.arange(Ls, dtype=np.float32)[:, None]
    t = pos / np.float32(max(Ls - 1, 1))
    bands = np.linspace(1e-4, HY_BANDS - 1, HY_BANDS, dtype=np.float32)
    ang = bands * np.float32(2.0 * math.pi / Ls) * pos
    z = np.concatenate([t, np.cos(ang), -np.sin(ang)], axis=-1).astype(np.float32)
    mn = math.log(1e-2) / 1.5
    mx = math.log(1e-2) / 0.3
    deltas = np.linspace(mn, mx, HY_W_, dtype=np.float32)
    decay = np.exp(-t * np.abs(deltas)[None, :]).astype(np.float32)
    return np.ascontiguousarray(z.T), np.ascontiguousarray(decay)


class Cfg:
    def __init__(self, L, LC, depth, always_ctx=False, emit_xs=False):
        self.L, self.LC, self.depth = L, LC, depth
        self.always_ctx, self.emit_xs = always_ctx, emit_xs
        self.TT = L + LC
        self.tiles = [(s, min(TN, L - s)) for s in range(0, L, TN)] + [(L + s, min(TN, LC - s)) for s in range(0, LC, TN)]


class K:
    def __init__(self, nc, cfg, dbg=()):
        self.nc, self.cfg = nc, cfg
        self.P = Prog(nc)
        self.dbg = set(dbg)
        self.bufs = {}
        import contextlib
        self.st = contextlib.ExitStack()

    def dram(self, name, shape, dt=F32, kind=None):
        if kind is None:
            kind = "ExternalOutput" if (name in self.dbg or name.rsplit("_", 1)[0] in self.dbg) else "Internal"
        t = self.nc.dram_tensor(name, list(shape), dt, kind=kind)
        return t.ap()

    def sb(self, name, shape, dt=F32):
        return self.st.enter_context(self.nc.sbuf_tensor(name, list(shape), dt))

    def ps(self, name, shape, dt=F32):
        return self.st.enter_context(self.nc.psum_tensor(name, list(shape), dt))


class RowBlocks:
    def __init__(self, k, name, rows, cols, blk=512):
        self.blk = blk
        self.t = [k.dram(f"{name}_{i}", [min(blk, rows - i * blk), cols]) for i in range((rows + blk - 1) // blk)]

    def rows(self, r0, r1):
        b = r0 // self.blk
        assert (r1 - 1) // self.blk == b, (r0, r1)
        return self.t[b][r0 - b * self.blk:r1 - b * self.blk, :]


class Arena:
    def __init__(self, k, words):
        self.t = k.sb("arena", [128, words])
        self.words = words
        self.off = 0

    def reset(self, off=0):
        self.off = off

    def alloc(self, shape, dt=F32):
        n = 1
        for d_ in shape[1:]:
            n *= d_
        size = 2 if dt == BF16 else 4
        w = (n * size + 3) // 4
        w = (w + 7) // 8 * 8
        assert self.off + w <= self.words, ("arena overflow", self.off, w, self.words)
        v = self.t[:shape[0], self.off:self.off + w]
        self.off += w
        if dt != F32:
            v = v.bitcast(dt)
        v = v[:, 0:n]
        if len(shape) == 3:
            v = v.rearrange("p (a b) -> p a b", b=shape[2])
        elif len(shape) == 4:
            v = v.rearrange("p (a b c) -> p a b c", b=shape[2], c=shape[3])
        return v


def r32(ap):
    return ap


def f32(ap):
    return ap.bitcast(F32)


def build(nc, cfg, dbg=(), stop_after=None):
    k = K(nc, cfg, dbg)
    P = k.P
    L, LC, TT, depth = cfg.L, cfg.LC, cfg.TT, cfg.depth
    ein = lambda name, shape: nc.dram_tensor(name, list(shape), F32, kind="ExternalInput").ap()
    x_in = ein("x_fm", [D, TT])
    cvec = ein("cvec", [128, KD, 2])
    ada_w = ein("ada_w", [depth, D, N_MOD * D])
    ada_b = ein("ada_b", [depth, 128, N_MOD * KD])
    n1g = ein("norm1_g", [depth, 128, KD])
    n2g = ein("norm2_g", [depth, 128, KD])
    w_in = ein("w_in", [depth, D, D_IN])
    out = nc.dram_tensor("out_fm", [D, L], F32, kind="ExternalOutput").ap()

    xs = k.dram("xs", [D, TT], kind=("ExternalOutput" if cfg.emit_xs else None))
    pr = RowBlocks(k, "pr", D_IN, TT)

    ones = k.sb("ones", [128, 128], BF16); Bones = Buf()
    P.op(POOL, lambda e: e.memset(ones[:], 1.0), w=[Bones])

    epst = k.sb("epst", [128, 1]); Beps = Buf()
    P.op(POOL, lambda e: e.memset(epst[:], EPS), w=[Beps])
    modT = k.sb("modT", [128, depth, N_MOD * KD, 2]); Bmod = Buf()
    cv = k.sb("cv", [128, KD, 2]); Bcv = Buf()
    sv = k.sb("sv", [128, KD, 2], BF16); Bsv = Buf()
    adab = k.sb("adab", [128, depth, N_MOD * KD]); Badab = Buf()
    P.dma(SP, cv[:], cvec, w=[Bcv])
    P.dma(SP, adab[:], ada_b.rearrange("l p j -> p l j"), w=[Badab])
    P.op(ACT, lambda e: e.activation(out=sv[:], in_=cv[:], func=AF.Silu), r=[Bcv], w=[Bsv])
    NB = 512
    A = Arena(k, 36000)
    wbt = [A.alloc([128, KD, 512], BF16) for i in range(2)]
    Bwbt = [Buf(), Buf()]
    awt, Bawt = wbt, Bwbt
    psm = [k.ps(f"psm{i}", [128, 512]) for i in range(8)]
    Bps = [Buf() for _ in range(8)]
    it = 0
    for l in range(depth):
        for nb in range(N_MOD * D // NB):
            wt, Bw = awt[it % 2], Bawt[it % 2]
            P.dma(POOL, wt[:], ada_w[l, :, nb * NB:(nb + 1) * NB].rearrange("(c p) n -> p c n", p=128), w=[Bw])
            pt, Bp = psm[it % 2], Bps[it % 2]
            for j in range(NB // 128):
                for c in range(KD):
                    P.op(PE, lambda e, j=j, c=c, wt=wt, pt=pt: e.matmul(pt[:, 2 * j:2 * j + 2], lhsT=r32(wt[:, c, j * 128:(j + 1) * 128]),
                                                                     rhs=r32(sv[:, c, :]), start=(c == 0), stop=(c == KD - 1)),
                         r=[Bw, Bsv], w=[Bp])
            j0 = nb * (NB // 128)
            P.op(DVE, lambda e, pt=pt, l=l, j0=j0: e.tensor_tensor(
                out=modT[:, l, j0:j0 + NB // 128, :], in0=pt[:, 0:2 * (NB // 128)].rearrange("p (j t) -> p j t", t=2),
                in1=adab[:, l, j0:j0 + NB // 128].unsqueeze(2).to_broadcast([128, NB // 128, 2]), op=ALU.add),
                r=[Bp, Badab], w=[Bmod])
            it += 1
    gsT = k.sb("gsT", [128, depth, 2, KD, 2]); Bgs = Buf()
    ng = k.sb("ng", [128, depth, 2, KD]); Bng = Buf()
    P.dma(SP, ng[:, :, 0, :], n1g.rearrange("l p c -> p l c"), w=[Bng])
    P.dma(SP, ng[:, :, 1, :], n2g.rearrange("l p c -> p l c"), w=[Bng])
    for l in range(depth):
        for wh, m in ((0, 1), (1, 4)):
            P.op(DVE, lambda e, l=l, wh=wh, m=m: e.scalar_tensor_tensor(
                out=gsT[:, l, wh, :, :], in0=modT[:, l, m * KD:(m + 1) * KD, :], scalar=1.0,
                in1=ng[:, l, wh, :].unsqueeze(2).to_broadcast([128, KD, 2]), op0=ALU.add, op1=ALU.mult),
                r=[Bmod, Bng], w=[Bgs])

    Bxs = Buf("xs")
    P.dma(SP, xs, x_in, w=[Bxs])

    ov0 = A.off
    xt = [A.alloc([128, KD, TN]) for i in range(1)]
    Bxt = [Buf()]
    ht = [A.alloc([128, KD, TN], BF16) for i in range(2)]
    Bht = [Buf(), Buf()]
    ov1 = A.off
    rstd = A.alloc([128, TN]); Brstd = Buf()
    ost = [A.alloc([128, 4, TN]) for i in range(2)]
    Bost = [Buf(), Buf()]
    cnt = {"t": 0, "w": 0, "p": 0, "o": 0}

    def norm_mod(l, wh, ti, src, Bsrc):
        t0, tn = cfg.tiles[ti]
        col = 1 if t0 >= L else 0
        i = cnt["t"] % 2
        cnt["t"] += 1
        X, BX, H, BH = xt[0], Bxt[0], ht[i], Bht[i]
        sq, Bsq = H, BH
        P.dma(SP, X[:, :, :tn], src[:, t0:t0 + tn].rearrange("(c p) t -> p c t", p=128), r=[Bsrc], w=[BX])
        P.op(ACT, lambda e: e.activation(out=sq[:, :, :tn], in_=X[:, :, :tn], func=AF.Square), r=[BX], w=[Bsq])
        pi = cnt["p"] % 8
        cnt["p"] += 1
        pt, Bp = psm[pi], Bps[pi]
        for c in range(KD):
            P.op(PE, lambda e, c=c: e.matmul(pt[:, :tn], lhsT=r32(ones[:]), rhs=r32(sq[:, c, :tn]), start=(c == 0), stop=(c == KD - 1)),
                 r=[Bones, Bsq], w=[Bp])
        P.op(ACT, lambda e: e.activation(out=rstd[:, :tn], in_=pt[:, :tn], func=AF.Sqrt, scale=1.0 / D, bias=epst[:, 0:1]),
             r=[Bp, Beps], w=[Brstd])
        P.op(DVE, lambda e: e.reciprocal(out=rstd[:, :tn], in_=rstd[:, :tn]), r=[Brstd], w=[Brstd])
        P.op(POOL, lambda e: e.tensor_tensor(out=H[:, :, :tn], in0=X[:, :, :tn], in1=rstd[:, :tn].unsqueeze(1).to_broadcast([128, KD, tn]), op=ALU.mult),
             r=[BX, Brstd], w=[BH])
        sh_m = 0 if wh == 0 else 3
        for c in range(KD):
            P.op(DVE, lambda e, c=c: e.tensor_scalar(out=H[:, c, :tn], in0=H[:, c, :tn], scalar1=gsT[:, l, wh, c, col:col + 1],
                                                     scalar2=modT[:, l, sh_m * KD + c, col:col + 1], op0=ALU.mult, op1=ALU.add),
                 r=[BH, Bgs, Bmod], w=[BH])
        return H, BH, t0, tn

    def linear(H, BH, Kc, tn, w2d, N, evac, sink, wtiles=None, nbw=512, tm=None):
        wt_, Bwt_ = wtiles if wtiles is not None else (wbt, Bwbt)
        nj = nbw // 128
        for nb in range(N // nbw):
            wi = cnt["w"] % len(wt_)
            cnt["w"] += 1
            W, BW = wt_[wi], Bwt_[wi]
            P.dma(POOL, W[:, :Kc, :nbw], w2d[:, nb * nbw:(nb + 1) * nbw].rearrange("(c p) n -> p c n", p=128), w=[BW])
            if tm is not None and nb in tm:
                for s_ in range((tn + 127) // 128):
                    ts = min(128, tn - s_ * 128)
                    pi = cnt["p"] % 8
                    cnt["p"] += 1
                    pt, Bp = psm[pi], Bps[pi]
                    for c in range(Kc):
                        P.op(PE, lambda e, c=c, pt=pt, W=W, s_=s_, ts=ts: e.matmul(pt[:ts, :nbw], lhsT=H[:, c, s_ * 128:s_ * 128 + ts], rhs=W[:, c, :nbw],
                                                                               start=(c == 0), stop=(c == Kc - 1)), r=[BW, BH], w=[Bp])
                    tm[nb](s_, ts, pt, Bp)
                continue
            oi = cnt["o"] % 2
            cnt["o"] += 1
            O, BO = ost[oi], Bost[oi]
            for j in range(nj):
                pi = cnt["p"] % 8
                cnt["p"] += 1
                pt, Bp = psm[pi], Bps[pi]
                for c in range(Kc):
                    P.op(PE, lambda e, j=j, c=c, pt=pt, W=W: e.matmul(pt[:, :tn], lhsT=W[:, c, j * 128:(j + 1) * 128], rhs=H[:, c, :tn],
                                                                   start=(c == 0), stop=(c == Kc - 1)), r=[BW, BH], w=[Bp])
                evac(nb * nj + j, j, pt, Bp, O, BO)
            sink(nb, O, BO)

    def evac_copy(tn):
        def f(jg, j, pt, Bp, O, BO):
            if j % 2 == 0:
                P.op(ACT, lambda e: e.copy(out=O[:, j, :tn], in_=pt[:, :tn]), r=[Bp], w=[BO])
            else:
                P.op(DVE, lambda e: e.tensor_copy(out=O[:, j, :tn], in_=pt[:, :tn]), r=[Bp], w=[BO])
        return f

    def evac_resid(l, m, col, tn, XO, BXO):
        def f(jg, j, pt, Bp, O, BO):
            P.op(DVE, lambda e: e.scalar_tensor_tensor(out=O[:, j, :tn], in0=pt[:, :tn], scalar=modT[:, l, m * KD + jg, col:col + 1],
                                                       in1=XO[:, j, :tn], op0=ALU.mult, op1=ALU.add), r=[Bp, Bmod, BXO], w=[BO])
        return f

    w_out = ein("w_out", [depth, D, D])
    w_up = ein("ffn_w_up", [depth, D, 2 * D_FF])
    w_dn = ein("ffn_w_down", [depth, D_FF, D])
    fcw = ein("ffn_conv_w", [depth, 128, 3, D_FF // 128])
    fcb = ein("ffn_conv_b", [depth, 128, D_FF // 128])
    fng = ein("final_norm_g", [128, KD])
    gu = RowBlocks(k, "gu", 2 * D_FF, TT); Bgu = Buf("gu", multi=True)
    asc = RowBlocks(k, "asc", D_FF, TT); Basc = Buf("asc", multi=True)
    ym = k.dram("ym", [D, TT]); Bym = Buf("ym", multi=True)
    KF = D_FF // 128
    cw = k.sb("cw", [128, depth, 3, KF]); Bcw = Buf()
    cb = k.sb("cb", [128, depth, KF]); Bcb = Buf()
    fg = k.sb("fg", [128, KD]); Bfg = Buf()
    P.dma(SP, cw[:], fcw.rearrange("l p t c -> p l t c"), w=[Bcw])
    P.dma(SP, cb[:], fcb.rearrange("l p c -> p l c"), w=[Bcb])
    P.dma(SP, fg[:], fng, w=[Bfg])
    xo = A.alloc([128, 4, TN]); Bxo = Buf()
    sv_off = A.off
    A.reset(ov0)
    at = A.alloc([128, KF, TN], BF16); Bat = Buf()
    wdn = [A.alloc([128, KF, 128], BF16) for i in range(2)]; Bwdn = [Buf(), Buf()]
    assert A.off <= ov1 + 600, (A.off, ov1)
    A.reset(max(sv_off, A.off))
    gt = A.alloc([128, TN + 2]); Bgt = Buf()
    ut = A.alloc([128, TN]); But = Buf()
    u2 = A.alloc([128, TN]); Bu2 = Buf()
    u3 = A.alloc([128, TN]); Bu3 = Buf()
    print("arena backbone words", A.off)
    GC = 2.0 * math.sqrt(2.0 / math.pi)

    def resid_linear(l, m, src_tile, Bsrc_tile, Kc, t0, tn, w2d, wtiles=None, nbw=512):
        col = 1 if t0 >= L else 0
        nj = nbw // 128

        def run():
            state = {}

            def evac(jg, j, pt, Bp, O, BO):
                if j == 0:
                    nb = jg // nj
                    P.dma(SP, xo[:, :nj, :tn], xs[nb * nbw:(nb + 1) * nbw, t0:t0 + tn].rearrange("(j p) t -> p j t", p=128), r=[Bxs], w=[Bxo])
                evac_resid(l, m, col, tn, xo, Bxo)(jg, j, pt, Bp, O, BO)

            def sink(nb, O, BO):
                P.dma(SP, xs[nb * nbw:(nb + 1) * nbw, t0:t0 + tn].rearrange("(j p) t -> p j t", p=128), O[:, :nj, :tn], r=[BO], w=[Bxs])
            linear(src_tile, Bsrc_tile, Kc, tn, w2d, D, evac, sink, wtiles=wtiles, nbw=nbw)
        run()

    def ffn_act(l, t0, tn, c, lo, hi):
        if lo > t0 - 1:
            P.op(POOL, lambda e: e.memset(gt[:, 0:1], 0.0), w=[Bgt])
        if hi < t0 + tn + 1:
            P.op(POOL, lambda e: e.memset(gt[:, tn + 1:tn + 2], 0.0), w=[Bgt])
        P.dma(SP, gt[:, lo - (t0 - 1):hi - (t0 - 1)], gu.rows(c * 128, (c + 1) * 128)[:, lo:hi], r=[Bgu], w=[Bgt])
        P.dma(SP, u3[:, :tn], gu.rows(D_FF + c * 128, D_FF + (c + 1) * 128)[:, t0:t0 + tn], r=[Bgu], w=[Bu3])
        P.op(DVE, lambda e, c=c: e.tensor_scalar(out=ut[:, :tn], in0=gt[:, 0:tn], scalar1=cw[:, l, 0, c:c + 1], scalar2=cb[:, l, c:c + 1],
                                                 op0=ALU.mult, op1=ALU.add), r=[Bgt, Bcw, Bcb], w=[But])
        P.op(DVE, lambda e, c=c: e.scalar_tensor_tensor(out=ut[:, :tn], in0=gt[:, 1:tn + 1], scalar=cw[:, l, 1, c:c + 1], in1=ut[:, :tn],
                                                        op0=ALU.mult, op1=ALU.add), r=[Bgt, Bcw, But], w=[But])
        P.op(DVE, lambda e, c=c: e.scalar_tensor_tensor(out=ut[:, :tn], in0=gt[:, 2:tn + 2], scalar=cw[:, l, 2, c:c + 1], in1=ut[:, :tn],
                                                        op0=ALU.mult, op1=ALU.add), r=[Bgt, Bcw, But], w=[But])
        P.op(POOL, lambda e: e.tensor_tensor(out=u2[:, :tn], in0=ut[:, :tn], in1=ut[:, :tn], op=ALU.mult), r=[But], w=[Bu2])
        P.op(POOL, lambda e: e.tensor_scalar(out=u2[:, :tn], in0=u2[:, :tn], scalar1=0.044715, scalar2=1.0, op0=ALU.mult, op1=ALU.add), r=[Bu2], w=[Bu2])
        P.op(POOL, lambda e: e.tensor_tensor(out=u2[:, :tn], in0=u2[:, :tn], in1=ut[:, :tn], op=ALU.mult), r=[Bu2, But], w=[Bu2])
        P.op(ACT, lambda e: e.activation(out=u2[:, :tn], in_=u2[:, :tn], func=AF.Sigmoid, scale=GC), r=[Bu2], w=[Bu2])
        P.op(DVE, lambda e: e.tensor_tensor(out=u3[:, :tn], in0=u3[:, :tn], in1=ut[:, :tn], op=ALU.mult), r=[Bu3, But], w=[Bu3])
        P.op(DVE, lambda e: e.tensor_tensor(out=u3[:, :tn], in0=u3[:, :tn], in1=u2[:, :tn], op=ALU.mult), r=[Bu3, Bu2], w=[Bu3])
        P.dma(SP, asc.rows(c * 128, (c + 1) * 128)[:, t0:t0 + tn], u3[:, :tn], r=[Bu3], w=[Basc])

    Bout = Buf("out", multi=True)

    def final_tile(t0, tn):
        X, BX = xt[0], Bxt[0]
        i = cnt["t"] % 2
        cnt["t"] += 1
        Hs, BHs = ht[i], Bht[i]
        P.dma(SP, X[:, :, :tn], xs[:, t0:t0 + tn].rearrange("(c p) t -> p c t", p=128), r=[Bxs], w=[BX])
        P.op(ACT, lambda e: e.activation(out=Hs[:, :, :tn], in_=X[:, :, :tn], func=AF.Square), r=[BX], w=[BHs])
        pi = cnt["p"] % 8
        cnt["p"] += 1
        pt, Bp = psm[pi], Bps[pi]
        for c in range(KD):
            P.op(PE, lambda e, c=c, pt=pt, Hs=Hs: e.matmul(pt[:, :tn], lhsT=ones[:], rhs=Hs[:, c, :tn], start=(c == 0), stop=(c == KD - 1)),
                 r=[Bones, BHs], w=[Bp])
        P.op(ACT, lambda e, pt=pt: e.activation(out=rstd[:, :tn], in_=pt[:, :tn], func=AF.Sqrt, scale=1.0 / D, bias=epst[:, 0:1]), r=[Bp, Beps], w=[Brstd])
        P.op(DVE, lambda e: e.reciprocal(out=rstd[:, :tn], in_=rstd[:, :tn]), r=[Brstd], w=[Brstd])
        P.op(POOL, lambda e: e.tensor_tensor(out=X[:, :, :tn], in0=X[:, :, :tn], in1=rstd[:, :tn].unsqueeze(1).to_broadcast([128, KD, tn]), op=ALU.mult),
             r=[BX, Brstd], w=[BX])
        P.op(DVE, lambda e: e.tensor_tensor(out=X[:, :, :tn], in0=X[:, :, :tn], in1=fg[:, :].unsqueeze(2).to_broadcast([128, KD, tn]), op=ALU.mult),
             r=[BX, Bfg], w=[BX])
        P.dma(SP, out[:, t0:t0 + tn].rearrange("(c p) t -> p c t", p=128), X[:, :, :tn], r=[BX], w=[Bout])

    vt = k.dram("vt", [TT, NA_W]); Bvt = Buf("vt", multi=True)
    hyt = k.dram("hyt", [TT, 3 * HY_W]); Bhyt = Buf("hyt", multi=True)
    aT = RowBlocks(k, "aT", NA_W, TT); BaT = Buf("aT", multi=True)
    Bpr = Buf("pr", multi=True)

    rows = L // 64
    NVAR = len(na_variants(rows))
    vidx = {v: i for i, v in enumerate(na_variants(rows))}
    NCt = LC // 128
    rpbg = ein("rpbg", [depth, NH, 128, NVAR * 320])
    maskc = ein("maskc", [128, NVAR * 320])
    ident_in = ein("ident", [128, 128])
    identb = k.sb("identb", [128, 128], BF16); Bident = Buf()
    P.dma(POOL, identb[:], ident_in, w=[Bident])
    SCALE = 128.0 ** -0.5

    def attention(l, upd_ctx):
        P.barrier()
        A.reset(0)
        qT = A.alloc([128, TT], BF16); BqT = Buf()
        kT = A.alloc([128, TT], BF16); BkT = Buf()
        V = A.alloc([128, TT // 128, 128], BF16); BV = Buf()
        Tst = A.alloc([128, NVAR * 320]); BTst = Buf()
        msk = A.alloc([128, NVAR * 320]); Bmsk = Buf()
        Tb = A.alloc([128, NVAR, 320], BF16); BTb = Buf()
        Et = [A.alloc([128, 320 + NCt * 64], BF16) for _ in range(2)]; BEt = [Buf(), Buf()]
        rec = [A.alloc([128, 64]) for _ in range(2)]; Brec = [Buf(), Buf()]
        arow = [A.alloc([128, 512]) for _ in range(2)]; Barow = [Buf(), Buf()]
        P.dma(SP, msk, maskc, w=[Bmsk])
        st = {"i": 0}

        def pv_norm(tiles, Etile, BE, dst_cols, h, flush, ncols_done):
            pi = cnt["p"] % 8
            cnt["p"] += 1
            po, Bpo = psm[pi], Bps[pi]
            n = len(tiles)
            for i, (vi_, ec) in enumerate(tiles):
                P.op(PE, lambda e, vi_=vi_, ec=ec, i=i: e.matmul(po[:, 0:64], lhsT=V[:, vi_, :], rhs=Etile[:, ec:ec + 64], start=(i == 0), stop=(i == n - 1)),
                     r=[BV, BE], w=[Bpo])
            for i, (vi_, ec) in enumerate(tiles):
                P.op(PE, lambda e, ec=ec, i=i: e.matmul(po[:, 64:128], lhsT=ones[:], rhs=Etile[:, ec:ec + 64], start=(i == 0), stop=(i == n - 1)),
                     r=[Bones, BE], w=[Bpo])
            j = st["i"] % 2
            R_, BR_ = rec[j], Brec[j]
            P.op(DVE, lambda e: e.reciprocal(out=R_[:, :], in_=po[:, 64:128]), r=[Bpo], w=[BR_])
            slot, ar_i = dst_cols
            AR, BAR = arow[ar_i], Barow[ar_i]
            P.op(DVE, lambda e: e.tensor_tensor(out=AR[:, slot * 64:(slot + 1) * 64], in0=po[:, 0:64], in1=R_[:, :], op=ALU.mult), r=[Bpo, BR_], w=[BAR])
            if flush is not None:
                c0, ncol = flush
                P.dma(SP, aT.rows(h * 128, (h + 1) * 128)[:, c0:c0 + ncol], AR[:, 0:ncol], r=[BAR], w=[BaT])

        def attn_row(h, r):
            r0, base, var = na_row_info(r, rows)
            vi = vidx[var]
            pi = cnt["p"] % 8
            cnt["p"] += 1
            pt, Bp = psm[pi], Bps[pi]
            q = qT[:, r * 64:(r + 1) * 64]
            for t in range(NCt):
                P.op(PE, lambda e, t=t: e.matmul(pt[:, 320 + t * 64:320 + (t + 1) * 64], lhsT=kT[:, L + t * 128:L + (t + 1) * 128], rhs=q, start=True, stop=True),
                     r=[BkT, BqT], w=[Bp])
            P.op(PE, lambda e: e.matmul(pt[:, 0:320], lhsT=identb[:], rhs=Tb[:, vi, :], start=True, stop=False), r=[Bident, BTb], w=[Bp])
            for t in range(5):
                k0 = (base + 2 * t) * 64
                P.op(PE, lambda e, t=t, k0=k0: e.matmul(pt[:, t * 64:(t + 1) * 64], lhsT=kT[:, k0:k0 + 128], rhs=q, start=False, stop=(t == 4)),
                     r=[BkT, BqT], w=[Bp])
            j = st["i"] % 2
            st["i"] += 1
            E, BE = Et[j], BEt[j]
            P.op(ACT, lambda e: e.activation(out=E[:, :], in_=pt[:, 0:320 + NCt * 64], func=AF.Exp, scale=SCALE), r=[Bp], w=[BE])
            tiles = [(base // 2 + t, t * 64) for t in range(5)] + [(L // 128 + t, 320 + t * 64) for t in range(NCt)]
            slot = r % 8
            ar_i = (r // 8) % 2
            flush = ((r - slot) * 64, (slot + 1) * 64) if (slot == 7 or r == rows - 1) else None
            pv_norm(tiles, E, BE, (slot, ar_i), h, flush, None)

        def attn_ctx_row(h, jq):
            pi = cnt["p"] % 8
            cnt["p"] += 1
            pt, Bp = psm[pi], Bps[pi]
            q = qT[:, L + jq * 64:L + (jq + 1) * 64]
            for t in range(NCt):
                P.op(PE, lambda e, t=t: e.matmul(pt[:, t * 64:(t + 1) * 64], lhsT=kT[:, L + t * 128:L + (t + 1) * 128], rhs=q, start=True, stop=True),
                     r=[BkT, BqT], w=[Bp])
            j = st["i"] % 2
            st["i"] += 1
            E, BE = Et[j], BEt[j]
            P.op(ACT, lambda e: e.activation(out=E[:, 0:NCt * 64], in_=pt[:, 0:NCt * 64], func=AF.Exp, scale=SCALE), r=[Bp], w=[BE])
            tiles = [(L // 128 + t, t * 64) for t in range(NCt)]
            nq = LC // 64
            flush = (L, nq * 64) if jq == nq - 1 else None
            pv_norm(tiles, E, BE, (jq, 0), h, flush, None)

        for h in range(NH):
            P.dma(POOL, qT, pr.rows(h * 128, (h + 1) * 128), r=[Bpr], w=[BqT])
            P.dma(POOL, kT, pr.rows(NA_W + h * 128, NA_W + (h + 1) * 128), r=[Bpr], w=[BkT])
            nvt = TT // 128
            for s0_ in range(0, nvt, 64):
                s1_ = min(nvt, s0_ + 64)
                P.dma(POOL, V[:, s0_:s1_, :], vt[s0_ * 128:s1_ * 128, h * 128:(h + 1) * 128].rearrange("(s p) d -> p s d", p=128), r=[Bvt], w=[BV])
            P.dma(SP, Tst, rpbg[l, h], w=[BTst])
            P.op(DVE, lambda e: e.scalar_tensor_tensor(out=Tb[:, :, :].rearrange("p a b -> p (a b)"), in0=Tst, scalar=float(128.0 ** 0.5), in1=msk,
                                                       op0=ALU.mult, op1=ALU.add), r=[BTst, Bmsk], w=[BTb])
            for r in range(rows):
                attn_row(h, r)
            if upd_ctx:
                assert LC // 64 <= 8
                for jq in range(LC // 64):
                    attn_ctx_row(h, jq)
        P.barrier()
        A.reset(0)


    mixers_done = False

    N1 = L // 128
    assert L % 128 == 0 and N1 <= 128 and LC == 256
    ftab = {}
    Bftab = Buf()
    for nm_, shp in (("C1", [128, 128]), ("mS1", [128, 128]), ("mC1", [128, 128]), ("C128", [128, 128]), ("S128", [128, 128]),
                     ("C256", [128, 2, 256]), ("S256", [128, 2, 256]), ("mS256", [128, 2, 256]), ("mC256", [128, 2, 256]),
                     ("CLc", [128, 2, 256]), ("mSLc", [128, 2, 256]), ("Chf", [128, 2, 512]), ("mShf", [128, 2, 512]),
                     ("Chi", [128, 4, 256]), ("mShi", [128, 4, 256])):
        src_ = ein("ft_" + nm_, shp)
        tl = k.sb("fts_" + nm_, shp, F32)
        P.dma(SP, tl[:], src_, w=[Bftab])
        ftab[nm_] = tl
    twf_in = ein("ft_twf", [128, 128, 3]); twh_in = ein("ft_twh", [128, 256, 3]); twi_in = ein("ft_twi", [128, 2, N1, 3])
    twf = k.sb("twf", [128, 128, 3]); twh = k.sb("twh", [128, 256, 3]); twi = k.sb("twi", [128, 2, N1, 3])
    P.dma(SP, twf[:], twf_in, w=[Bftab]); P.dma(SP, twh[:], twh_in, w=[Bftab]); P.dma(SP, twi[:], twi_in, w=[Bftab])
    identf = k.sb("identf", [128, 128]); Bidf = Buf()
    P.dma(SP, identf[:], ident_in, w=[Bidf])
    mixg_in = ein("mix_g_bc", [depth, 128, D])
    zf = k.dram("zf", [TT, 2 * FN_W]); Bzf = Buf("zf", multi=True)
    fa = k.dram("fa", [128, 128, 2 * FN_W]); Bfa = Buf("fa", multi=True)
    tmn = k.dram("tmn", [TT, FN_W]); Btmn = Buf("tmn", multi=True)
    ym_rb = RowBlocks(k, "ymx", D, TT)
    Bymx = Buf("ymx", multi=True)

    def psum():
        pi = cnt["p"] % 8
        cnt["p"] += 1
        return psm[pi], Bps[pi]

    def tw_apply(dst, Bdst, prr, Bprr, pii, Bpii, tcos, tsin, tmsin, npart, tmp, Btmp, engs=(DVE, POOL)):
        e0, e1 = engs
        P.op(DVE, lambda e: e.tensor_scalar(out=tmp[:npart, 0:512], in0=prr[:npart, 0:512], scalar1=tcos, scalar2=None, op0=ALU.mult), r=[Bprr, Bftab], w=[Btmp])
        P.op(DVE, lambda e: e.scalar_tensor_tensor(out=dst[:npart, 0:512], in0=pii[:npart, 0:512], scalar=tsin, in1=tmp[:npart, 0:512], op0=ALU.mult, op1=ALU.add),
             r=[Bpii, Btmp, Bftab], w=[Bdst])
        P.op(DVE, lambda e: e.tensor_scalar(out=tmp[:npart, 512:1024], in0=pii[:npart, 0:512], scalar1=tcos, scalar2=None, op0=ALU.mult), r=[Bpii, Bftab], w=[Btmp])
        P.op(DVE, lambda e: e.scalar_tensor_tensor(out=dst[:npart, 512:1024], in0=prr[:npart, 0:512], scalar=tmsin, in1=tmp[:npart, 512:1024], op0=ALU.mult, op1=ALU.add),
             r=[Bprr, Btmp, Bftab], w=[Bdst])

    def group_norm_store(l, y, By, npart, gsl, row0, gbc, Bgbc, sq_t, Bsq, ssq, Bssq, width=512):
        P.op(ACT, lambda e: e.activation(out=sq_t[:npart, :width], in_=y[:npart, :width], func=AF.Square, accum_out=ssq[:npart, 0:1]), r=[By], w=[Bsq, Bssq])
        P.op(ACT, lambda e: e.activation(out=ssq[:npart, 0:1], in_=ssq[:npart, 0:1], func=AF.Sqrt, scale=1.0 / width, bias=epst[:npart, 0:1]), r=[Bssq, Beps], w=[Bssq])
        P.op(DVE, lambda e: e.reciprocal(out=ssq[:npart, 0:1], in_=ssq[:npart, 0:1]), r=[Bssq], w=[Bssq])
        P.op(DVE, lambda e: e.scalar_tensor_tensor(out=sq_t[:npart, :width], in0=y[:npart, :width], scalar=ssq[:npart, 0:1], in1=gbc[:npart, gsl], op0=ALU.mult, op1=ALU.mult),
             r=[By, Bssq, Bgbc], w=[Bsq])
        P.dma(SP, tmn[row0:row0 + npart, 0:width], sq_t[:npart, :width], r=[Bsq], w=[Btmn])

    def tm2fm(ntok0, ntok, dst_row0, nch=4):
        tl = A.alloc([128, nch * 128]); Btl = Buf()
        ot = A.alloc([128, nch, 128]); Bot = Buf()
        for s_ in range(ntok // 128):
            t0_ = ntok0 + s_ * 128
            P.dma(SP, tl, tmn[t0_:t0_ + 128, 0:nch * 128], r=[Btmn], w=[Btl])
            pt, Bp = psum()
            for c in range(nch):
                P.op(PE, lambda e, c=c, pt=pt: e.matmul(pt[:, c * 128:(c + 1) * 128], lhsT=tl[:, c * 128:(c + 1) * 128], rhs=identf[:], start=True, stop=True),
                     r=[Btl, Bidf], w=[Bp])
            P.op(ACT, lambda e, pt=pt: e.copy(out=ot[:, :, :].rearrange("p a b -> p (a b)"), in_=pt[:, 0:nch * 128]), r=[Bp], w=[Bot])
            P.dma(SP, ym_rb.rows(dst_row0, dst_row0 + nch * 128)[:, t0_:t0_ + 128].rearrange("(c p) t -> p c t", p=128), ot, r=[Bot], w=[Bymx])

    def fourier(l, upd_ctx):
        P.barrier()
        A.reset(0)
        gbc = A.alloc([128, D]); Bgbc = Buf()
        P.dma(SP, gbc, mixg_in[l], w=[Bgbc])
        fT = A.alloc([128, 4, 512]); BfT = Buf()
        zt = [A.alloc([128, 1024]) for _ in range(2)]; Bzt = [Buf(), Buf()]
        d1 = [A.alloc([128, 1024]) for _ in range(2)]; Bd1 = [Buf(), Buf()]
        ap_ = [A.alloc([128, 1024]) for _ in range(2)]; Bap = [Buf(), Buf()]
        tmp = A.alloc([128, 1024]); Btmp = Buf()
        yt = [A.alloc([128, 512]) for _ in range(2)]; Byt = [Buf(), Buf()]
        sq_t = A.alloc([128, 512]); Bsq = Buf()
        ssq = A.alloc([128, 2]); Bssq = Buf()
        dc = A.alloc([128, 2, 1024]); Bdc = Buf()
        sc_f = 1.0 / math.sqrt(L * 128.0)
        sc_c = 1.0 / math.sqrt(LC * 128.0)
        ntok = TT if upd_ctx else L
        it = 0
        for tb0 in range(0, ntok, 512):
            tbn = min(512, ntok - tb0)
            P.dma(SP, fT[:, :, :tbn], pr.rows(3 * NA_W, 3 * NA_W + FN_W)[:, tb0:tb0 + tbn].rearrange("(g p) t -> p g t", p=128), r=[Bpr], w=[BfT])
            for s_ in range(tbn // 128):
                pa, Bpa = psum()
                pb, Bpb = psum()
                for g in range(4):
                    P.op(PE, lambda e, g=g, s_=s_, pa=pa: e.matmul(pa[:, g * 128:(g + 1) * 128], lhsT=fT[:, g, s_ * 128:(s_ + 1) * 128], rhs=ftab["C128"][:], start=True, stop=True),
                         r=[BfT, Bftab], w=[Bpa])
                    P.op(PE, lambda e, g=g, s_=s_, pb=pb: e.matmul(pb[:, g * 128:(g + 1) * 128], lhsT=fT[:, g, s_ * 128:(s_ + 1) * 128], rhs=ftab["S128"][:], start=True, stop=True),
                         r=[BfT, Bftab], w=[Bpb])
                Z, BZ = zt[it % 2], Bzt[it % 2]
                it += 1
                P.op(ACT, lambda e, pa=pa, Z=Z: e.copy(out=Z[:, 0:512], in_=pa[:, 0:512]), r=[Bpa], w=[BZ])
                P.op(DVE, lambda e, pb=pb, Z=Z: e.tensor_copy(out=Z[:, 512:1024], in_=pb[:, 0:512]), r=[Bpb], w=[BZ])
                P.dma(SP, zf[tb0 + s_ * 128:tb0 + (s_ + 1) * 128, :], Z, r=[BZ], w=[Bzf])
        zfv = zf[0:L, :].rearrange("(a b) c -> b a c", b=128)
        for n2 in range(128):
            Dt, BD = d1[n2 % 2], Bd1[n2 % 2]
            P.dma(SP, Dt[:N1, :], zfv[n2], r=[Bzf], w=[BD])
            pr_, Bpr_ = psum()
            pi_, Bpi_ = psum()
            P.op(PE, lambda e, Dt=Dt, pr_=pr_: e.matmul(pr_[:, 0:512], lhsT=ftab["C1"][:N1, :], rhs=Dt[:N1, 0:512], start=True, stop=False), r=[BD, Bftab], w=[Bpr_])
            P.op(PE, lambda e, Dt=Dt, pr_=pr_: e.matmul(pr_[:, 0:512], lhsT=ftab["mS1"][:N1, :], rhs=Dt[:N1, 512:1024], start=False, stop=True), r=[BD, Bftab], w=[Bpr_])
            P.op(PE, lambda e, Dt=Dt, pi_=pi_: e.matmul(pi_[:, 0:512], lhsT=ftab["mS1"][:N1, :], rhs=Dt[:N1, 0:512], start=True, stop=False), r=[BD, Bftab], w=[Bpi_])
            P.op(PE, lambda e, Dt=Dt, pi_=pi_: e.matmul(pi_[:, 0:512], lhsT=ftab["mC1"][:N1, :], rhs=Dt[:N1, 512:1024], start=False, stop=True), r=[BD, Bftab], w=[Bpi_])
            AP_, BA = ap_[n2 % 2], Bap[n2 % 2]
            tw_apply(AP_, BA, pr_, Bpr_, pi_, Bpi_, twf[:N1, n2, 0:1], twf[:N1, n2, 1:2], twf[:N1, n2, 2:3], N1, tmp, Btmp)
            P.dma(SP, fa[0:N1, n2, :], AP_[:N1, :], r=[BA], w=[Bfa])
        tmv = tmn[0:L, :].rearrange("(k2 k1) c -> k1 k2 c", k1=N1)
        for k1 in range(N1):
            Dt, BD = d1[k1 % 2], Bd1[k1 % 2]
            P.dma(SP, Dt[:, :], fa[k1], r=[Bfa], w=[BD])
            py, Bpy = psum()
            P.op(PE, lambda e, Dt=Dt, py=py: e.matmul(py[:, 0:512], lhsT=ftab["C128"][:], rhs=Dt[:, 0:512], start=True, stop=False), r=[BD, Bftab], w=[Bpy])
            P.op(PE, lambda e, Dt=Dt, py=py: e.matmul(py[:, 0:512], lhsT=ftab["S128"][:], rhs=Dt[:, 512:1024], start=False, stop=True), r=[BD, Bftab], w=[Bpy])
            Y, BY = yt[k1 % 2], Byt[k1 % 2]
            P.op(ACT, lambda e, py=py, Y=Y: e.activation(out=Y[:, :], in_=py[:, 0:512], func=AF.Copy, scale=sc_f), r=[Bpy], w=[BY])
            P.op(ACT, lambda e, Y=Y: e.activation(out=sq_t[:, :], in_=Y[:, :], func=AF.Square, accum_out=ssq[:, 0:1]), r=[BY], w=[Bsq, Bssq])
            P.op(ACT, lambda e: e.activation(out=ssq[:, 0:1], in_=ssq[:, 0:1], func=AF.Sqrt, scale=1.0 / 512, bias=epst[:, 0:1]), r=[Bssq, Beps], w=[Bssq])
            P.op(DVE, lambda e: e.reciprocal(out=ssq[:, 0:1], in_=ssq[:, 0:1]), r=[Bssq], w=[Bssq])
            P.op(DVE, lambda e, Y=Y: e.scalar_tensor_tensor(out=sq_t[:, :], in0=Y[:, :], scalar=ssq[:, 0:1], in1=gbc[:, NA_W:NA_W + FN_W], op0=ALU.mult, op1=ALU.mult),
                 r=[BY, Bssq, Bgbc], w=[Bsq])
            P.dma(SP, tmv[k1], sq_t[:, :], r=[Bsq], w=[Btmn])
        if upd_ctx:
            P.dma(SP, dc[:, :, :], zf[L:L + LC, :].rearrange("(a p) c -> p a c", p=128), r=[Bzf], w=[Bdc])
            for mc in range(2):
                py, Bpy = psum()
                i_ = 0
                for kc_ in range(2):
                    for tab, c0 in (("CLc", 0), ("mSLc", 512)):
                        P.op(PE, lambda e, kc_=kc_, tab=tab, c0=c0, py=py, i_=i_, mc=mc: e.matmul(py[:, 0:512], lhsT=ftab[tab][:, kc_, mc * 128:(mc + 1) * 128],
                                                                                                 rhs=dc[:, kc_, c0:c0 + 512], start=(i_ == 0), stop=(i_ == 3)), r=[Bdc, Bftab], w=[Bpy])
                        i_ += 1
                Y, BY = yt[mc % 2], Byt[mc % 2]
                P.op(ACT, lambda e, py=py, Y=Y: e.activation(out=Y[:, :], in_=py[:, 0:512], func=AF.Copy, scale=sc_c), r=[Bpy], w=[BY])
                group_norm_store(l, Y, BY, 128, slice(NA_W, NA_W + FN_W), L + mc * 128, gbc, Bgbc, sq_t, Bsq, ssq, Bssq)
        tm2fm(0, ntok, NA_W)
        P.barrier()
        A.reset(0)


    HN = N1 // 2 if N1 >= 2 else 1
    hcw_in = ein("hy_cw_bc", [depth, 128, 4, 3 * HY_W])
    hbias_in = ein("hy_bias_bc", [depth, 128, 2, HY_W])
    hw1_in = ein("hy_w1", [depth, 33, 64]); hw2_in = ein("hy_w2", [depth, 64, 64]); hw3_in = ein("hy_w3", [depth, 64, 4 * HY_W])
    hvec_in = ein("hy_vec", [depth, 64, 3])
    zT_in = {"lat": ein("hy_zT_lat", [33, L]), "ctx": ein("hy_zT_ctx", [33, LC])}
    dec_in = {"lat": ein("hy_dec_lat", [L, HY_W]), "ctx": ein("hy_dec_ctx", [LC, HY_W])}
    hu = k.dram("hu", [TT, 3 * HY_W]); Bhu = Buf("hu", multi=True)
    hfil = {"lat": k.dram("hfil_lat", [L, 4 * HY_W]), "ctx": k.dram("hfil_ctx", [LC, 4 * HY_W])}; Bhfil = Buf("hfil", multi=True)
    fa2 = [k.dram(f"fa2_{i}", [128, 256, 2 * HY_W]) for i in range(2)]; Bfa2 = [Buf("fa2a", multi=True), Buf("fa2b", multi=True)]
    fbd = k.dram("fbd", [256, 128, 2 * HY_W]); Bfbd = Buf("fbd", multi=True)
    ksp = [k.dram(f"ksp{o}", [128, 256, 2 * HY_W]) for o in range(2)]; Bksp = [Buf("ksp0", multi=True), Buf("ksp1", multi=True)]
    z2d = k.dram("z2d", [TT, HY_W]); Bz2d = Buf("z2d", multi=True)
    TWO_PI = 2.0 * math.pi
    negpi = k.sb("negpi", [128, 1]); Bnegpi = Buf()
    P.op(POOL, lambda e: e.memset(negpi[:], -math.pi), w=[Bnegpi])

    def hy_dwconv(l, ntok):
        A.reset(0)
        cwb = A.alloc([128, 4, 3 * HY_W]); Bcwb = Buf()
        P.dma(SP, cwb, hcw_in[l], w=[Bcwb])
        tl3 = [A.alloc([128, 3 * HY_W]) for _ in range(3)]; Btl3 = [Buf(), Buf(), Buf()]
        acc = A.alloc([128, 3 * HY_W]); Bacc = Buf()
        tq = A.alloc([128, 3 * HY_W]); Btq = Buf()

        def tile(t0_, s0, s1):
            for j, dlt in enumerate((-1, 0, 1)):
                lo, hi = max(t0_ + dlt, s0), min(t0_ + dlt + 128, s1)
                if hi - lo < 128:
                    P.op(POOL, lambda e, j=j: e.memset(tl3[j][:, :], 0.0), w=[Btl3[j]])
                P.dma(SP, tl3[j][lo - (t0_ + dlt):hi - (t0_ + dlt), :], hyt[lo:hi, :], r=[Bhyt], w=[Btl3[j]])
            P.op(DVE, lambda e: e.tensor_tensor(out=acc[:, :], in0=tl3[0][:, :], in1=cwb[:, 0, :], op=ALU.mult), r=[Btl3[0], Bcwb], w=[Bacc])
            P.op(POOL, lambda e: e.tensor_tensor(out=tq[:, :], in0=tl3[1][:, :], in1=cwb[:, 1, :], op=ALU.mult), r=[Btl3[1], Bcwb], w=[Btq])
            P.op(DVE, lambda e: e.tensor_tensor(out=acc[:, :], in0=acc[:, :], in1=tq[:, :], op=ALU.add), r=[Bacc, Btq], w=[Bacc])
            P.op(POOL, lambda e: e.tensor_tensor(out=tq[:, :], in0=tl3[2][:, :], in1=cwb[:, 2, :], op=ALU.mult), r=[Btl3[2], Bcwb], w=[Btq])
            P.op(DVE, lambda e: e.tensor_tensor(out=acc[:, :], in0=acc[:, :], in1=tq[:, :], op=ALU.add), r=[Bacc, Btq], w=[Bacc])
            P.op(DVE, lambda e: e.tensor_tensor(out=acc[:, :], in0=acc[:, :], in1=cwb[:, 3, :], op=ALU.add), r=[Bacc, Bcwb], w=[Bacc])
            P.dma(SP, hu[t0_:t0_ + 128, :], acc[:, :], r=[Bacc], w=[Bhu])
        for t0_ in range(0, ntok, 128):
            s0, s1 = (0, L) if t0_ < L else (L, TT)
            tile(t0_, s0, s1)

    def hy_filter(l, which):
        Ls, r0 = (L, 0) if which == "lat" else (LC, L)
        A.reset(0)
        w1 = A.alloc([33, 64]); w2 = A.alloc([64, 64]); w3 = A.alloc([64, 4 * HY_W]); hv = A.alloc([64, 3]); fb = A.alloc([64, 3]); Bw = Buf()
        P.dma(SP, w1, hw1_in[l], w=[Bw]); P.dma(SP, w2, hw2_in[l], w=[Bw]); P.dma(SP, w3, hw3_in[l], w=[Bw]); P.dma(SP, hv, hvec_in[l], w=[Bw])
        P.op(DVE, lambda e: e.tensor_scalar(out=fb[:, 0:2], in0=hv[:, 1:3], scalar1=hv[:, 0:1], scalar2=1.0 / TWO_PI, op0=ALU.mult, op1=ALU.mult), r=[Bw], w=[Bw])
        P.op(DVE, lambda e: e.tensor_scalar(out=fb[:, 2:3], in0=hv[:, 0:1], scalar1=1.0 / TWO_PI, scalar2=None, op0=ALU.mult), r=[Bw], w=[Bw])
        zt_ = A.alloc([33, 512]); Bz_ = Buf()
        h1 = A.alloc([64, 512]); Bh1 = Buf()
        h2 = A.alloc([64, 512]); Bh2 = Buf()
        dct = A.alloc([128, HY_W]); Bdct = Buf()
        ho = A.alloc([128, 4 * HY_W]); Bho = Buf()

        ni = A.alloc([64, 512]).bitcast(mybir.dt.int32); Bni = Buf()
        nf = A.alloc([64, 512]); Bnf = Buf()

        def sin_layer(ps_, Bps_, dst, Bdst, j, n):
            P.op(DVE, lambda e: e.tensor_scalar(out=dst[:, :n], in0=ps_[:64, :n], scalar1=fb[:, 2:3], scalar2=fb[:, j:j + 1], op0=ALU.mult, op1=ALU.add), r=[Bps_, Bw], w=[Bdst])
            P.op(DVE, lambda e: e.tensor_copy(out=ni[:, :n], in_=dst[:, :n]), r=[Bdst], w=[Bni])
            P.op(DVE, lambda e: e.tensor_copy(out=nf[:, :n], in_=ni[:, :n]), r=[Bni], w=[Bnf])
            P.op(DVE, lambda e: e.tensor_tensor(out=dst[:, :n], in0=dst[:, :n], in1=nf[:, :n], op=ALU.subtract), r=[Bdst, Bnf], w=[Bdst])
            P.op(ACT, lambda e: e.activation(out=dst[:, :n], in_=dst[:, :n], func=AF.Sin, scale=TWO_PI), r=[Bdst], w=[Bdst])

        def blk(p0, n):
            P.dma(SP, zt_[:, :n], zT_in[which][:, p0:p0 + n], w=[Bz_])
            ps1, Bps1 = psum()
            P.op(PE, lambda e: e.matmul(ps1[:64, :n], lhsT=w1[:, :], rhs=zt_[:, :n], start=True, stop=True), r=[Bw, Bz_], w=[Bps1])
            sin_layer(ps1, Bps1, h1, Bh1, 0, n)
            ps2, Bps2 = psum()
            P.op(PE, lambda e: e.matmul(ps2[:64, :n], lhsT=w2[:, :], rhs=h1[:, :n], start=True, stop=True), r=[Bw, Bh1], w=[Bps2])
            sin_layer(ps2, Bps2, h2, Bh2, 1, n)
            for s_ in range(n // 128):
                P.dma(SP, dct[:, :], dec_in[which][p0 + s_ * 128:p0 + (s_ + 1) * 128, :], w=[Bdct])
                for cb_ in range(4):
                    ps3, Bps3 = psum()
                    P.op(PE, lambda e, ps3=ps3, cb_=cb_, s_=s_: e.matmul(ps3[:, 0:512], lhsT=h2[:, s_ * 128:(s_ + 1) * 128], rhs=w3[:, cb_ * 512:(cb_ + 1) * 512], start=True, stop=True),
                         r=[Bw, Bh2], w=[Bps3])
                    P.op(DVE, lambda e, ps3=ps3, cb_=cb_: e.tensor_tensor(out=ho[:, cb_ * 512:(cb_ + 1) * 512], in0=ps3[:, 0:512], in1=dct[:, :], op=ALU.mult), r=[Bps3, Bdct], w=[Bho])
                if p0 + s_ * 128 == 0:
                    P.op(POOL, lambda e: e.memset(ho[0:1, 512:1024], 0.0), w=[Bho])
                    P.op(POOL, lambda e: e.memset(ho[0:1, 1536:2048], 0.0), w=[Bho])
                rr = p0 + s_ * 128
                P.dma(SP, hfil[which][rr:rr + 128, :], ho[:, :], r=[Bho], w=[Bhfil])
        for p0 in range(0, Ls, 512):
            blk(p0, min(512, Ls - p0))

    def fft_stage1(src_rows, Bsrc, c0, fa_i, d1, Bd1, ap_, Bap, tmp, Btmp):
        for n2 in range(256):
            Dt, BD = d1[n2 % 2], Bd1[n2 % 2]
            P.dma(SP, Dt[:HN, 0:512], src_rows(n2)[:, c0:c0 + 512], r=[Bsrc], w=[BD])
            pr_, Bpr_ = psum()
            pi_, Bpi_ = psum()
            P.op(PE, lambda e, Dt=Dt, pr_=pr_: e.matmul(pr_[:, 0:512], lhsT=ftab["C1"][:HN, :], rhs=Dt[:HN, 0:512], start=True, stop=True), r=[BD, Bftab], w=[Bpr_])
            P.op(PE, lambda e, Dt=Dt, pi_=pi_: e.matmul(pi_[:, 0:512], lhsT=ftab["mS1"][:HN, :], rhs=Dt[:HN, 0:512], start=True, stop=True), r=[BD, Bftab], w=[Bpi_])
            AP_, BA = ap_[n2 % 2], Bap[n2 % 2]
            tw_apply(AP_, BA, pr_, Bpr_, pi_, Bpi_, twh[:N1, n2, 0:1], twh[:N1, n2, 1:2], twh[:N1, n2, 2:3], N1, tmp, Btmp)
            P.dma(SP, fa2[fa_i][0:N1, n2, :], AP_[:N1, :], r=[BA], w=[Bfa2[fa_i]])

    def stage2_mm(pzr, Bpzr, pzi, Bpzi, Dt, BD, mc, first, last, conj):
        tr = [("C256", 0), ("S256", 512)]
        ti = [("S256", 0), ("mC256", 512)] if conj else [("mS256", 0), ("C256", 512)]
        for (pz, Bpz, terms) in ((pzr, Bpzr, tr), (pzi, Bpzi, ti)):
            n = 0
            for kc_ in range(2):
                for tab, co in terms:
                    st_ = first and n == 0
                    sp_ = last and n == 3
                    P.op(PE, lambda e, pz=pz, tab=tab, kc_=kc_, co=co, st_=st_, sp_=sp_: e.matmul(pz[:, 0:512], lhsT=ftab[tab][:, kc_, mc * 128:(mc + 1) * 128],
                                                                                                  rhs=Dt[:, kc_, co:co + 512], start=st_, stop=sp_), r=[BD, Bftab], w=[Bpz])
                    n += 1

    def hy_kspec(l):
        A.reset(0)
        d1 = [A.alloc([128, 1024]) for _ in range(2)]; Bd1 = [Buf(), Buf()]
        ap_ = [A.alloc([128, 1024]) for _ in range(2)]; Bap = [Buf(), Buf()]
        tmp = A.alloc([128, 1024]); Btmp = Buf()
        dA = [A.alloc([128, 2, 1024]) for _ in range(2)]; BdA = [Buf(), Buf()]
        ko = [A.alloc([128, 2, 1024]) for _ in range(2)]; Bko = [Buf(), Buf()]
        hv_ = lambda n2: None
        for o in range(2):
            for di in range(2):
                c0 = o * 1024 + di * 512
                src = lambda n2: hfil["lat"].rearrange("(a b) c -> b a c", b=256)[n2][0:HN]
                fft_stage1(src, Bhfil, c0, di, d1, Bd1, ap_, Bap, tmp, Btmp)
            for k1 in range(N1):
                for di in range(2):
                    P.dma(SP, dA[di][:, :, :], fa2[di][k1].rearrange("(a p) c -> p a c", p=128), r=[Bfa2[di]], w=[BdA[di]])
                KO, BKO = ko[k1 % 2], Bko[k1 % 2]
                for mc in range(2):
                    pzr, Bpzr = psum()
                    pzi, Bpzi = psum()
                    stage2_mm(pzr, Bpzr, pzi, Bpzi, dA[0], BdA[0], mc, True, False, False)
                    stage2_mm(pzr, Bpzr, pzi, Bpzi, dA[1], BdA[1], mc, False, True, True)
                    P.op(ACT, lambda e, pzr=pzr, KO=KO, mc=mc: e.copy(out=KO[:, mc, 0:512], in_=pzr[:, 0:512]), r=[Bpzr], w=[BKO])
                    P.op(DVE, lambda e, pzi=pzi, KO=KO, mc=mc: e.tensor_copy(out=KO[:, mc, 512:1024], in_=pzi[:, 0:512]), r=[Bpzi], w=[BKO])
                P.dma(SP, ksp[o][k1].rearrange("(a p) c -> p a c", p=128), KO[:, :, :], r=[BKO], w=[Bksp[o]])

    def hy_conv(l, o, src_rows, Bsrc, c0, epilogue):
        A.reset(0)
        d1 = [A.alloc([128, 1024]) for _ in range(2)]; Bd1 = [Buf(), Buf()]
        ap_ = [A.alloc([128, 1024]) for _ in range(2)]; Bap = [Buf(), Buf()]
        tmp = A.alloc([128, 1024]); Btmp = Buf()
        dA = A.alloc([128, 2, 1024]); BdA = Buf()
        kt = A.alloc([128, 2, 1024]); Bkt = Buf()
        zt_ = A.alloc([128, 2, 1024]); Bzt_ = Buf()
        yy = A.alloc([128, 2, 1024]); Byy = Buf()
        t1 = A.alloc([128, 2, 512]); Bt1 = Buf()
        t2 = A.alloc([128, 2, 512]); Bt2 = Buf()
        arena_mark = A.off
        fft_stage1(src_rows, Bsrc, c0, 0, d1, Bd1, ap_, Bap, tmp, Btmp)
        for k1 in range(N1):
            P.dma(SP, dA[:, :, :], fa2[0][k1].rearrange("(a p) c -> p a c", p=128), r=[Bfa2[0]], w=[BdA])
            P.dma(SP, kt[:, :, :], ksp[o][k1].rearrange("(a p) c -> p a c", p=128), r=[Bksp[o]], w=[Bkt])
            for mc in range(2):
                pzr, Bpzr = psum()
                pzi, Bpzi = psum()
                stage2_mm(pzr, Bpzr, pzi, Bpzi, dA, BdA, mc, True, True, False)
                P.op(ACT, lambda e, pzr=pzr, mc=mc: e.copy(out=zt_[:, mc, 0:512], in_=pzr[:, 0:512]), r=[Bpzr], w=[Bzt_])
                P.op(ACT, lambda e, pzi=pzi, mc=mc: e.copy(out=zt_[:, mc, 512:1024], in_=pzi[:, 0:512]), r=[Bpzi], w=[Bzt_])
            Zr, Zi, Kr, Ki = zt_[:, :, 0:512], zt_[:, :, 512:1024], kt[:, :, 0:512], kt[:, :, 512:1024]
            P.op(DVE, lambda e: e.tensor_tensor(out=t1[:, :, :], in0=Zr, in1=Kr, op=ALU.mult), r=[Bzt_, Bkt], w=[Bt1])
            P.op(POOL, lambda e: e.tensor_tensor(out=t2[:, :, :], in0=Zi, in1=Ki, op=ALU.mult), r=[Bzt_, Bkt], w=[Bt2])
            P.op(DVE, lambda e: e.tensor_tensor(out=yy[:, :, 0:512], in0=t1[:, :, :], in1=t2[:, :, :], op=ALU.subtract), r=[Bt1, Bt2], w=[Byy])
            P.op(DVE, lambda e: e.tensor_tensor(out=t1[:, :, :], in0=Zr, in1=Ki, op=ALU.mult), r=[Bzt_, Bkt], w=[Bt1])
            P.op(POOL, lambda e: e.tensor_tensor(out=t2[:, :, :], in0=Zi, in1=Kr, op=ALU.mult), r=[Bzt_, Bkt], w=[Bt2])
            P.op(DVE, lambda e: e.tensor_tensor(out=yy[:, :, 512:1024], in0=t1[:, :, :], in1=t2[:, :, :], op=ALU.add), r=[Bt1, Bt2], w=[Byy])
            for nc_ in range(2):
                pbr, Bpbr = psum()
                pbi, Bpbi = psum()
                for (pz, Bpz, terms) in ((pbr, Bpbr, [("C256", 0), ("mS256", 512)]), (pbi, Bpbi, [("S256", 0), ("C256", 512)])):
                    n = 0
                    for kc_ in range(2):
                        for tab, co in terms:
                            P.op(PE, lambda e, pz=pz, tab=tab, kc_=kc_, co=co, n=n, nc_=nc_: e.matmul(pz[:, 0:512], lhsT=ftab[tab][:, kc_, nc_ * 128:(nc_ + 1) * 128],
                                                                                                  rhs=yy[:, kc_, co:co + 512], start=(n == 0), stop=(n == 3)), r=[Byy, Bftab], w=[Bpz])
                            n += 1
                AP_, BA = ap_[nc_], Bap[nc_]
                tw_apply(AP_, BA, pbr, Bpbr, pbi, Bpbi, twi[:, nc_, k1, 0:1], twi[:, nc_, k1, 2:3], twi[:, nc_, k1, 1:2], 128, tmp, Btmp)
                P.dma(SP, fbd[nc_ * 128:(nc_ + 1) * 128, k1, :], AP_[:, :], r=[BA], w=[Bfbd])
        for n2 in range(256):
            Dt, BD = d1[n2 % 2], Bd1[n2 % 2]
            P.dma(SP, Dt[:N1, :], fbd[n2, 0:N1, :], r=[Bfbd], w=[BD])
            py, Bpy = psum()
            P.op(PE, lambda e, Dt=Dt, py=py: e.matmul(py[:, 0:512], lhsT=ftab["C1"][:N1, :], rhs=Dt[:N1, 0:512], start=True, stop=False), r=[BD, Bftab], w=[Bpy])
            P.op(PE, lambda e, Dt=Dt, py=py: e.matmul(py[:, 0:512], lhsT=ftab["mS1"][:N1, :], rhs=Dt[:N1, 512:1024], start=False, stop=True), r=[BD, Bftab], w=[Bpy])
            epilogue(n2, py, Bpy)


    def hy_ctx(l):
        A.reset(0)
        hfc = A.alloc([128, 2, 4 * HY_W]); Bhfc = Buf()
        P.dma(SP, hfc, hfil["ctx"].rearrange("(a p) c -> p a c", p=128), r=[Bhfil], w=[Bhfc])
        sd = A.alloc([128, 2, 2, HY_W]); Bsd = Buf()
        KK = A.alloc([128, 4, 2, HY_W]); BKK = Buf()
        ZZ = A.alloc([128, 4, 2, HY_W]); BZZ = Buf()
        YY = A.alloc([128, 4, 2, HY_W]); BYY = Buf()
        t1 = A.alloc([128, 4, HY_W]); Bt1 = Buf()
        t2 = A.alloc([128, 4, HY_W]); Bt2 = Buf()
        uc = A.alloc([128, 2, 3 * HY_W]); Buc = Buf()
        P.dma(SP, uc, hu[L:L + LC, :].rearrange("(a p) c -> p a c", p=128), r=[Bhu], w=[Buc])
        zin = A.alloc([128, 2, HY_W]); Bzin = Buf()
        z2c = A.alloc([128, 2, HY_W]); Bz2c = Buf()
        hb = A.alloc([128, 2, HY_W]); Bhb = Buf()
        P.dma(SP, hb, hbias_in[l], w=[Bhb])
        gbc = A.alloc([128, HY_W]); Bgbc = Buf()
        P.dma(SP, gbc, mixg_in[l][:, NA_W + FN_W:D], w=[Bgbc])
        sq_t = A.alloc([128, HY_W]); Bsq = Buf()
        ssq = A.alloc([128, 2]); Bssq = Buf()
        tt_ = A.alloc([128, HY_W]); Btt = Buf()
        inv_n = 1.0 / (2.0 * LC)

        def fwd(dst, Bdst, srcs):
            (fr, fi), Bs = srcs
            for kc_ in range(4):
                for part, tab, fsrc in ((0, "Chf", fr), (1, "mShf", fi)):
                    pz, Bpz = psum()
                    for nch in range(2):
                        P.op(PE, lambda e, pz=pz, tab=tab, nch=nch, kc_=kc_, fsrc=fsrc: e.matmul(pz[:, 0:512], lhsT=ftab[tab][:, nch, kc_ * 128:(kc_ + 1) * 128], rhs=fsrc(nch),
                                                                                           start=(nch == 0), stop=(nch == 1)), r=[Bs, Bftab], w=[Bpz])
                    if part == 0:
                        P.op(ACT, lambda e, pz=pz, kc_=kc_: e.copy(out=dst[:, kc_, 0, :], in_=pz[:, 0:512]), r=[Bpz], w=[Bdst])
                    else:
                        P.op(DVE, lambda e, pz=pz, kc_=kc_: e.tensor_copy(out=dst[:, kc_, 1, :], in_=pz[:, 0:512]), r=[Bpz], w=[Bdst])

        def conv(o, zsrc, Bzsrc, epi):
            c0 = o * 2 * HY_W
            P.op(DVE, lambda e: e.tensor_tensor(out=sd[:, :, 0, :], in0=hfc[:, :, c0:c0 + 512], in1=hfc[:, :, c0 + 512:c0 + 1024], op=ALU.add), r=[Bhfc], w=[Bsd])
            P.op(POOL, lambda e: e.tensor_tensor(out=sd[:, :, 1, :], in0=hfc[:, :, c0:c0 + 512], in1=hfc[:, :, c0 + 512:c0 + 1024], op=ALU.subtract), r=[Bhfc], w=[Bsd])
            fwd(KK, BKK, ((lambda nch: sd[:, nch, 0, :], lambda nch: sd[:, nch, 1, :]), Bsd))
            fwd(ZZ, BZZ, ((zsrc, zsrc), Bzsrc))
            Zr, Zi, Kr, Ki = ZZ[:, :, 0, :], ZZ[:, :, 1, :], KK[:, :, 0, :], KK[:, :, 1, :]
            P.op(DVE, lambda e: e.tensor_tensor(out=t1[:, :, :], in0=Zr, in1=Kr, op=ALU.mult), r=[BZZ, BKK], w=[Bt1])
            P.op(POOL, lambda e: e.tensor_tensor(out=t2[:, :, :], in0=Zi, in1=Ki, op=ALU.mult), r=[BZZ, BKK], w=[Bt2])
            P.op(DVE, lambda e: e.tensor_tensor(out=YY[:, :, 0, :], in0=t1[:, :, :], in1=t2[:, :, :], op=ALU.subtract), r=[Bt1, Bt2], w=[BYY])
            P.op(DVE, lambda e: e.tensor_tensor(out=t1[:, :, :], in0=Zr, in1=Ki, op=ALU.mult), r=[BZZ, BKK], w=[Bt1])
            P.op(POOL, lambda e: e.tensor_tensor(out=t2[:, :, :], in0=Zi, in1=Kr, op=ALU.mult), r=[BZZ, BKK], w=[Bt2])
            P.op(DVE, lambda e: e.tensor_tensor(out=YY[:, :, 1, :], in0=t1[:, :, :], in1=t2[:, :, :], op=ALU.add), r=[Bt1, Bt2], w=[BYY])
            for nch in range(2):
                py, Bpy = psum()
                n = 0
                for kc_ in range(4):
                    for tab, part in (("Chi", 0), ("mShi", 1)):
                        P.op(PE, lambda e, py=py, tab=tab, kc_=kc_, part=part, n=n, nch=nch: e.matmul(py[:, 0:512], lhsT=ftab[tab][:, kc_, nch * 128:(nch + 1) * 128],
                                                                                                  rhs=YY[:, kc_, part, :], start=(n == 0), stop=(n == 7)), r=[BYY, Bftab], w=[Bpy])
                        n += 1
                epi(nch, py, Bpy)

        def epi1(nch, py, Bpy):
            P.op(POOL, lambda e: e.tensor_tensor(out=tt_[:, :], in0=uc[:, nch, 0:512], in1=hb[:, 0, :], op=ALU.mult), r=[Buc, Bhb], w=[Btt])
            P.op(DVE, lambda e: e.scalar_tensor_tensor(out=tt_[:, :], in0=py[:, 0:512], scalar=inv_n, in1=tt_[:, :], op0=ALU.mult, op1=ALU.add), r=[Bpy, Btt], w=[Btt])
            P.op(DVE, lambda e: e.tensor_tensor(out=z2c[:, nch, :], in0=tt_[:, :], in1=uc[:, nch, 512:1024], op=ALU.mult), r=[Btt, Buc], w=[Bz2c])

        def epi2(nch, py, Bpy):
            P.op(POOL, lambda e: e.tensor_tensor(out=tt_[:, :], in0=z2c[:, nch, :], in1=hb[:, 1, :], op=ALU.mult), r=[Bz2c, Bhb], w=[Btt])
            P.op(DVE, lambda e: e.scalar_tensor_tensor(out=tt_[:, :], in0=py[:, 0:512], scalar=inv_n, in1=tt_[:, :], op0=ALU.mult, op1=ALU.add), r=[Bpy, Btt], w=[Btt])
            P.op(DVE, lambda e: e.tensor_tensor(out=tt_[:, :], in0=tt_[:, :], in1=uc[:, nch, 1024:1536], op=ALU.mult), r=[Btt, Buc], w=[Btt])
            group_norm_store(l, tt_, Btt, 128, slice(0, HY_W), L + nch * 128, gbc, Bgbc, sq_t, Bsq, ssq, Bssq)
        conv(0, lambda nch: uc[:, nch, 0:512], Buc, epi1)
        conv(1, lambda nch: z2c[:, nch, :], Bz2c, epi2)

    def hyena(l, upd_ctx):
        P.barrier()
        ntok = TT if upd_ctx else L
        hy_dwconv(l, ntok)
        P.barrier()
        hy_filter(l, "lat")
        P.barrier()
        hy_kspec(l)
        P.barrier()
        inv_n = 1.0 / (2.0 * L)
        huv = hu[0:L, :].rearrange("(a b) c -> b a c", b=256)
        z2v = z2d[0:L, :].rearrange("(a b) c -> b a c", b=256)
        tmv = tmn[0:L, :].rearrange("(a b) c -> b a c", b=256)
        st1 = {}

        def ep1(n2, py, Bpy):
            if not st1:
                A.reset(A.off)
                st1["hb"] = A.alloc([128, 2, HY_W]); st1["Bhb"] = Buf()
                P.dma(SP, st1["hb"], hbias_in[l], w=[st1["Bhb"]])
                st1["vx"] = [A.alloc([128, 2 * HY_W]) for _ in range(2)]; st1["Bvx"] = [Buf(), Buf()]
                st1["t"] = [A.alloc([128, HY_W]) for _ in range(2)]; st1["Bt"] = [Buf(), Buf()]
            vx, Bvx = st1["vx"][n2 % 2], st1["Bvx"][n2 % 2]
            tt_, Btt = st1["t"][n2 % 2], st1["Bt"][n2 % 2]
            hb, Bhb = st1["hb"], st1["Bhb"]
            P.dma(SP, vx[:HN, :], huv[n2][0:HN, 0:2 * HY_W], r=[Bhu], w=[Bvx])
            P.op(POOL, lambda e: e.tensor_tensor(out=tt_[:HN, :], in0=vx[:HN, 0:512], in1=hb[:HN, 0, :], op=ALU.mult), r=[Bvx, Bhb], w=[Btt])
            P.op(DVE, lambda e: e.scalar_tensor_tensor(out=tt_[:HN, :], in0=py[:HN, 0:512], scalar=inv_n, in1=tt_[:HN, :], op0=ALU.mult, op1=ALU.add), r=[Bpy, Btt], w=[Btt])
            P.op(DVE, lambda e: e.tensor_tensor(out=tt_[:HN, :], in0=tt_[:HN, :], in1=vx[:HN, 512:1024], op=ALU.mult), r=[Btt, Bvx], w=[Btt])
            P.dma(SP, z2v[n2][0:HN, :], tt_[:HN, :], r=[Btt], w=[Bz2d])
        hy_conv(l, 0, lambda n2: huv[n2][0:HN], Bhu, 0, ep1)
        P.barrier()
        st2 = {}

        def ep2(n2, py, Bpy):
            if not st2:
                A.reset(A.off)
                st2["hb"] = A.alloc([128, 2, HY_W]); st2["Bhb"] = Buf()
                P.dma(SP, st2["hb"], hbias_in[l], w=[st2["Bhb"]])
                st2["gbc"] = A.alloc([128, HY_W]); st2["Bgbc"] = Buf()
                P.dma(SP, st2["gbc"], mixg_in[l][:, NA_W + FN_W:D], w=[st2["Bgbc"]])
                st2["zz"] = [A.alloc([128, HY_W]) for _ in range(2)]; st2["Bzz"] = [Buf(), Buf()]
                st2["xx"] = [A.alloc([128, HY_W]) for _ in range(2)]; st2["Bxx"] = [Buf(), Buf()]
                st2["t"] = [A.alloc([128, HY_W]) for _ in range(2)]; st2["Bt"] = [Buf(), Buf()]
                st2["sq"] = A.alloc([128, HY_W]); st2["Bsq"] = Buf()
                st2["ssq"] = A.alloc([128, 2]); st2["Bssq"] = Buf()
            zz, Bzz = st2["zz"][n2 % 2], st2["Bzz"][n2 % 2]
            xx, Bxx = st2["xx"][n2 % 2], st2["Bxx"][n2 % 2]
            tt_, Btt = st2["t"][n2 % 2], st2["Bt"][n2 % 2]
            hb, Bhb = st2["hb"], st2["Bhb"]
            P.dma(SP, zz[:HN, :], z2v[n2][0:HN, :], r=[Bz2d], w=[Bzz])
            P.dma(SP, xx[:HN, :], huv[n2][0:HN, 2 * HY_W:3 * HY_W], r=[Bhu], w=[Bxx])
            P.op(POOL, lambda e: e.tensor_tensor(out=tt_[:HN, :], in0=zz[:HN, :], in1=hb[:HN, 1, :], op=ALU.mult), r=[Bzz, Bhb], w=[Btt])
            P.op(DVE, lambda e: e.scalar_tensor_tensor(out=tt_[:HN, :], in0=py[:HN, 0:512], scalar=inv_n, in1=tt_[:HN, :], op0=ALU.mult, op1=ALU.add), r=[Bpy, Btt], w=[Btt])
            P.op(DVE, lambda e: e.tensor_tensor(out=tt_[:HN, :], in0=tt_[:HN, :], in1=xx[:HN, :], op=ALU.mult), r=[Btt, Bxx], w=[Btt])
            sq_t, Bsq, ssq, Bssq, gbc, Bgbc = st2["sq"], st2["Bsq"], st2["ssq"], st2["Bssq"], st2["gbc"], st2["Bgbc"]
            P.op(ACT, lambda e: e.activation(out=sq_t[:HN, :], in_=tt_[:HN, :], func=AF.Square, accum_out=ssq[:HN, 0:1]), r=[Btt], w=[Bsq, Bssq])
            P.op(ACT, lambda e: e.activation(out=ssq[:HN, 0:1], in_=ssq[:HN, 0:1], func=AF.Sqrt, scale=1.0 / HY_W, bias=epst[:HN, 0:1]), r=[Bssq, Beps], w=[Bssq])
            P.op(DVE, lambda e: e.reciprocal(out=ssq[:HN, 0:1], in_=ssq[:HN, 0:1]), r=[Bssq], w=[Bssq])
            P.op(DVE, lambda e: e.scalar_tensor_tensor(out=sq_t[:HN, :], in0=tt_[:HN, :], scalar=ssq[:HN, 0:1], in1=gbc[:HN, :], op0=ALU.mult, op1=ALU.mult),
                 r=[Btt, Bssq, Bgbc], w=[Bsq])
            P.dma(SP, tmv[n2][0:HN, :], sq_t[:HN, :], r=[Bsq], w=[Btmn])
        hy_conv(l, 1, lambda n2: z2v[n2][0:HN], Bz2d, 0, ep2)
        if upd_ctx:
            P.barrier()
            hy_filter(l, "ctx")
            P.barrier()
            hy_ctx(l)
        P.barrier()
        A.reset(0)
        tm2fm(0, ntok, NA_W + FN_W)
        P.barrier()
        A.reset(0)


    mgf_in = ein("mix_g_fm", [depth, 128, KD])
    mg = k.sb("mg", [128, depth, KD]); Bmg = Buf()
    P.dma(SP, mg[:], mgf_in.rearrange("l p c -> p l c"), w=[Bmg])

    def merge_tile(l, t0, tn):
        i = cnt["t"] % 2
        cnt["t"] += 1
        H, BH = ht[i], Bht[i]
        X, BX = xt[0], Bxt[0]
        for hb_ in range(2):
            P.dma(SP, X[:, hb_ * 4:(hb_ + 1) * 4, :tn], aT.rows(hb_ * 512, (hb_ + 1) * 512)[:, t0:t0 + tn].rearrange("(c p) t -> p c t", p=128), r=[BaT], w=[BX])
        P.op(ACT, lambda e: e.activation(out=H[:, 0:8, :tn], in_=X[:, 0:8, :tn], func=AF.Square), r=[BX], w=[BH])
        pt, Bp = psum()
        for c in range(8):
            P.op(PE, lambda e, c=c: e.matmul(pt[:, :tn], lhsT=ones[:], rhs=H[:, c, :tn], start=(c == 0), stop=(c == 7)), r=[Bones, BH], w=[Bp])
        P.op(ACT, lambda e: e.activation(out=rstd[:, :tn], in_=pt[:, :tn], func=AF.Sqrt, scale=1.0 / NA_W, bias=epst[:, 0:1]), r=[Bp, Beps], w=[Brstd])
        P.op(DVE, lambda e: e.reciprocal(out=rstd[:, :tn], in_=rstd[:, :tn]), r=[Brstd], w=[Brstd])
        P.op(POOL, lambda e: e.tensor_tensor(out=X[:, 0:8, :tn], in0=X[:, 0:8, :tn], in1=rstd[:, :tn].unsqueeze(1).to_broadcast([128, 8, tn]), op=ALU.mult),
             r=[BX, Brstd], w=[BX])
        for c in range(8):
            P.op(DVE, lambda e, c=c: e.tensor_scalar(out=H[:, c, :tn], in0=X[:, c, :tn], scalar1=mg[:, l, c:c + 1], scalar2=None, op0=ALU.mult), r=[BX, Bmg], w=[BH])
        for hb_ in range(2):
            r0_ = NA_W + hb_ * 512
            P.dma(POOL, H[:, 8 + hb_ * 4:8 + (hb_ + 1) * 4, :tn], ym_rb.rows(r0_, r0_ + 512)[:, t0:t0 + tn].rearrange("(c p) t -> p c t", p=128), r=[Bymx], w=[BH])
        resid_linear(l, 2, H, BH, KD, t0, tn, w_out[l])

    for l in range(depth):
        upd_ctx = (l < depth - 1) or cfg.always_ctx
        tiles_l = [ti for ti, (t0, tn) in enumerate(cfg.tiles) if (t0 < L or True)]
        for ti in tiles_l:
            H, BH, t0, tn = norm_mod(l, 0, ti, xs, Bxs)

            def sink(nb, O, BO, t0=t0, tn=tn):
                return P.dma(SP, pr.rows(nb * 512, (nb + 1) * 512)[:, t0:t0 + tn].rearrange("(j p) t -> p j t", p=128), O[:, :, :tn], r=[BO], w=[Bpr])
            def mk_tm(dst, col0, t0=t0):
                def f(s_, ts, pt, Bp):
                    oi = cnt["o"] % 2
                    cnt["o"] += 1
                    O, BO = ost[oi], Bost[oi]
                    Of = O[:, :, :].rearrange("p a b -> p (a b)")
                    P.op(ACT if s_ % 2 == 0 else DVE, (lambda e: e.copy(out=Of[:ts, 0:512], in_=pt[:ts, 0:512])) if s_ % 2 == 0 else
                         (lambda e: e.tensor_copy(out=Of[:ts, 0:512], in_=pt[:ts, 0:512])), r=[Bp], w=[BO])
                    P.dma(SP, dst[t0 + s_ * 128:t0 + s_ * 128 + ts, col0:col0 + 512], Of[:ts, 0:512], r=[BO], w=[Bvt if dst is vt else Bhyt])
                return f
            tm = {4: mk_tm(vt, 0), 5: mk_tm(vt, 512), 7: mk_tm(hyt, 0), 8: mk_tm(hyt, 512), 9: mk_tm(hyt, 1024)}
            linear(H, BH, KD, tn, w_in[l], D_IN, evac_copy(tn), sink, tm=tm)
        if stop_after == "A":
            break
        attention(l, upd_ctx)
        if stop_after == "B1":
            break
        fourier(l, upd_ctx)
        if stop_after == "B2":
            break
        hyena(l, upd_ctx)
        if stop_after == "B3":
            break
        P.barrier()
        for ti, (t0, tn) in enumerate(cfg.tiles):
            if t0 >= L and not upd_ctx:
                continue
            merge_tile(l, t0, tn)
        for ti, (t0, tn) in enumerate(cfg.tiles):
            if t0 >= L and not upd_ctx:
                continue
            H, BH, t0, tn = norm_mod(l, 1, ti, xs, Bxs)

            def sink(nb, O, BO, t0=t0, tn=tn):
                return P.dma(SP, gu.rows(nb * 512, (nb + 1) * 512)[:, t0:t0 + tn].rearrange("(j p) t -> p j t", p=128), O[:, :, :tn], r=[BO], w=[Bgu])
            linear(H, BH, KD, tn, w_up[l], 2 * D_FF, evac_copy(tn), sink)
        for ti, (t0, tn) in enumerate(cfg.tiles):
            if t0 >= L and not upd_ctx:
                continue
            s0, s1 = (0, L) if t0 < L else (L, TT)
            lo, hi = max(t0 - 1, s0), min(t0 + tn + 1, s1)
            for c in range(KF):
                ffn_act(l, t0, tn, c, lo, hi)
        P.barrier()
        for ti, (t0, tn) in enumerate(cfg.tiles):
            if t0 >= L and not upd_ctx:
                continue
            for bb in range(KF // 4):
                P.dma(POOL, at[:, bb * 4:(bb + 1) * 4, :tn], asc.rows(bb * 512, (bb + 1) * 512)[:, t0:t0 + tn].rearrange("(c p) t -> p c t", p=128), r=[Basc], w=[Bat])
            resid_linear(l, 5, at, Bat, KF, t0, tn, w_dn[l], wtiles=(wdn, Bwdn), nbw=128)
        P.barrier()
    if stop_after is None:
        for ti, (t0, tn) in enumerate(cfg.tiles):
            if t0 < L:
                final_tile(t0, tn)
        fin = k.sb("fin", [1, 8]); Bfin = Buf()
        last = P.dma(SP, fin[:], out[0:1, 0:8], r=[Bout, Bpr, Bgu, Basc, Bxs], w=[Bfin])
    else:
        fin = k.sb("fin", [1, 8]); Bfin = Buf()
        last = P.dma(SP, out[0:1, 0:8], xs[0:1, 0:8], r=[Bpr, Bxs, Bgu, Basc], w=[Bfin])
    P.emit(final_wait=[last])
    k.st.close()
    return nc


def _fm_vec(v):
    return np.ascontiguousarray(np.asarray(v, np.float32).reshape(-1, 128).T)


def host_shared(inp, cfg):
    depth = cfg.depth
    f32c = lambda a: np.ascontiguousarray(np.asarray(a, np.float32))
    rows = cfg.L // 64
    tabs = [na_tables(np.asarray(inp["na_rpb"][l], np.float32), rows) for l in range(depth)]
    NV = tabs[0][0].shape[2]
    d = dict(
        ada_w=f32c(inp["ada_w"]),
        ada_b=np.stack([_fm_vec(inp["ada_b"][l]) for l in range(depth)]),
        norm1_g=np.stack([_fm_vec(inp["norm1_g"][l]) for l in range(depth)]),
        norm2_g=np.stack([_fm_vec(inp["norm2_g"][l]) for l in range(depth)]),
        w_in=f32c(inp["w_in"]), w_out=f32c(inp["w_out"]), ffn_w_up=f32c(inp["ffn_w_up"]), ffn_w_down=f32c(inp["ffn_w_down"]),
        ffn_conv_w=np.stack([np.stack([_fm_vec(inp["ffn_conv_w"][l][t]) for t in range(3)], 1) for l in range(depth)]),
        ffn_conv_b=np.stack([_fm_vec(inp["ffn_conv_b"][l]) for l in range(depth)]),
        final_norm_g=_fm_vec(inp["final_norm_g"]),
        rpbg=np.stack([t[0].reshape(NH, 128, NV * 320) for t in tabs]),
        maskc=tabs[0][1].reshape(128, NV * 320),
        ident=np.eye(128, dtype=np.float32),
        mix_g_fm=np.stack([_fm_vec(inp["mix_norm_g"][l]) for l in range(depth)]),
        mix_g_bc=np.stack([np.broadcast_to(np.asarray(inp["mix_norm_g"][l], np.float32)[None, :], (128, D)) for l in range(depth)]),
    )
    for k_, v in fft_tables(cfg.L, cfg.LC).items():
        d["ft_" + k_] = v
    bc = lambda v: np.broadcast_to(np.asarray(v, np.float32)[None], (128,) + tuple(np.asarray(v).shape))
    d["hy_cw_bc"] = np.stack([bc(np.concatenate([inp["hy_conv_w"][l], inp["hy_conv_b"][l][None]], 0)) for l in range(depth)])
    d["hy_bias_bc"] = np.stack([bc(inp["hy_bias"][l]) for l in range(depth)])
    d["hy_w1"] = inp["hy_w1"]; d["hy_w2"] = inp["hy_w2"]; d["hy_w3"] = inp["hy_w3"]
    d["hy_vec"] = np.stack([np.stack([inp["hy_freq"][l], inp["hy_b1"][l], inp["hy_b2"][l]], -1) for l in range(depth)])
    for nm_, Ls_ in (("lat", cfg.L), ("ctx", cfg.LC)):
        zT_, dec_ = hyena_consts(Ls_)
        d["hy_zT_" + nm_] = zT_; d["hy_dec_" + nm_] = dec_
    return {k_: np.ascontiguousarray(v, dtype=np.float32) for k_, v in d.items()}


def host_core(inp, b):
    x_fm = np.ascontiguousarray(np.concatenate([np.asarray(inp["x"][b], np.float32), np.asarray(inp["ctx"][b], np.float32)], 0).T)
    cvec = np.ascontiguousarray(np.stack([np.asarray(inp["c"][b], np.float32).reshape(KD, 128).T,
                                          np.asarray(inp["c_ctx"], np.float32).reshape(KD, 128).T], -1))
    return dict(x_fm=x_fm, cvec=cvec)


def kernel(**inputs):
    inp = {k_: np.asarray(v) for k_, v in inputs.items()}
    B, L, _ = inp["x"].shape
    LC = inp["ctx"].shape[1]
    depth = inp["ada_w"].shape[0]
    cfg = Cfg(L, LC, 1, always_ctx=True, emit_xs=True)
    nc = bass.Bass("TRN2", target_bir_lowering=False)
    build(nc, cfg)
    per_layer = [k_ for k_, v in inp.items() if v.ndim >= 1 and v.shape[0] == depth and k_ not in ("x", "c", "ctx")]
    xs_cur = [host_core(inp, b)["x_fm"] for b in range(B)]
    cvecs = [host_core(inp, b)["cvec"] for b in range(B)]
    out = None
    for l in range(depth):
        inp_l = dict(inp)
        for k_ in per_layer:
            inp_l[k_] = inp[k_][l:l + 1]
        shared = host_shared(inp_l, cfg)
        in_maps = []
        for b in range(B):
            m = dict(shared)
            m.update(x_fm=xs_cur[b], cvec=cvecs[b])
            in_maps.append(m)
        res = run_bass_kernel_spmd(nc, in_maps, core_ids=list(range(B)))
        xs_cur = [np.ascontiguousarray(res.results[b]["xs"]) for b in range(B)]
        if l == depth - 1:
            out = np.stack([np.ascontiguousarray(res.results[b]["out_fm"].T) for b in range(B)], 0)
    return out.astype(np.float32)
```
